# Optimizing a Trainium2 kernel written in Bass

```python
import jax, jax.numpy as jnp
from jax import lax
import numpy as np

D_MODEL = 1024
BATCH = 8
SEQ = 4096
DEPTH = 4

GRID_W = 64
BLOCK_Q = 128
CHUNK = 128
ROPE_THETA = 10000.0
EPS = 1e-6
A_HEADS = 8
A_KV_HEADS = 2
A_HEAD_DIM = 64
A_WIDTH = A_HEADS * A_HEAD_DIM
A_KV_WIDTH = A_KV_HEADS * A_HEAD_DIM
B_GROUPS = 4
B_GROUP_DIM = 128
B_WIDTH = B_GROUPS * B_GROUP_DIM
M_HEADS = 4
M_HEAD_DIM = 128
M_WIDTH = M_HEADS * M_HEAD_DIM
MEM_LEN = 256
N_BRANCH = 3
BRANCH_WIDTH = 512
IN_SPLITS = (A_WIDTH, A_KV_WIDTH, A_KV_WIDTH, A_WIDTH,
             B_WIDTH, B_WIDTH, B_WIDTH,
             M_WIDTH, M_WIDTH,
             N_BRANCH * D_MODEL)
IN_WIDTH = sum(IN_SPLITS)

kernel_name = "hybrid_gqa_gmlp_memory_encoder"


def _split_points():
    pts, acc = [], 0
    for s in IN_SPLITS[:-1]:
        acc += s
        pts.append(acc)
    return pts


def rms_norm(x, g):
    xf = x.astype(jnp.float32)
    y = xf * lax.rsqrt(jnp.mean(xf * xf, axis=-1, keepdims=True) + EPS)
    return (y * g.astype(jnp.float32)).astype(x.dtype)


def layer_norm(x, g, b):
    xf = x.astype(jnp.float32)
    mu = jnp.mean(xf, axis=-1, keepdims=True)
    xc = xf - mu
    y = xc * lax.rsqrt(jnp.mean(xc * xc, axis=-1, keepdims=True) + EPS)
    return (y * g.astype(jnp.float32) + b.astype(jnp.float32)).astype(x.dtype)


def axial_rope_tables(seq):
    rows = seq // GRID_W
    row = jnp.repeat(jnp.arange(rows, dtype=jnp.float32), GRID_W)
    col = jnp.tile(jnp.arange(GRID_W, dtype=jnp.float32), rows)
    n_freq = A_HEAD_DIM // 4
    inv = ROPE_THETA ** (-jnp.arange(n_freq, dtype=jnp.float32) / n_freq)
    ang = jnp.stack([row[:, None] * inv, col[:, None] * inv], axis=1)
    return jnp.cos(ang), jnp.sin(ang)


def apply_axial_rope(x, cos, sin):
    b, s, h, d = x.shape
    x5 = x.reshape(b, s, h, 2, 2, d // 4)
    x1, x2 = x5[..., 0, :], x5[..., 1, :]
    c = cos[None, :, None].astype(x.dtype)
    sn = sin[None, :, None].astype(x.dtype)
    out = jnp.stack([x1 * c - x2 * sn, x2 * c + x1 * sn], axis=-2)
    return out.reshape(b, s, h, d)


def gqa_axial_attention(q, k, v, q_g, k_g, cos, sin):
    bsz, seq, _ = q.shape
    grp = A_HEADS // A_KV_HEADS
    q = apply_axial_rope(rms_norm(q.reshape(bsz, seq, A_HEADS, A_HEAD_DIM), q_g), cos, sin)
    k = apply_axial_rope(rms_norm(k.reshape(bsz, seq, A_KV_HEADS, A_HEAD_DIM), k_g), cos, sin)
    v = v.reshape(bsz, seq, A_KV_HEADS, A_HEAD_DIM)
    scale = A_HEAD_DIM ** -0.5
    nblk = seq // BLOCK_Q
    qb = q.reshape(bsz, nblk, BLOCK_Q, A_KV_HEADS, grp, A_HEAD_DIM).transpose(1, 0, 3, 4, 2, 5)
    kt = k.transpose(0, 2, 1, 3)
    vt = v.transpose(0, 2, 1, 3)

    def one_block(qblk):
        s = jnp.einsum('bkgqd,bksd->bkgqs', qblk, kt,
                       preferred_element_type=jnp.float32) * scale
        p = jax.nn.softmax(s, axis=-1)
        return jnp.einsum('bkgqs,bksd->bkgqd', p.astype(vt.dtype), vt)

    o = lax.map(one_block, qb)
    return o.transpose(1, 0, 4, 2, 3, 5).reshape(bsz, seq, A_WIDTH)


def chunked_spatial_gating(u, v, ln_g, ln_b, w_s, b_s):
    bsz, seq, _ = v.shape
    v = layer_norm(v, ln_g, ln_b)
    vc = v.reshape(bsz, seq // CHUNK, CHUNK, B_GROUPS, B_GROUP_DIM)
    mixed = jnp.einsum('gpq,bnqgc->bnpgc', w_s, vc) + b_s.T[None, None, :, :, None]
    return u * mixed.reshape(bsz, seq, B_WIDTH)


def memory_cross_attention(q, mem_n, w_kv):
    bsz, seq, _ = q.shape
    kv = mem_n @ w_kv
    k, v = jnp.split(kv, 2, axis=-1)
    k = k.reshape(bsz, -1, M_HEADS, M_HEAD_DIM)
    v = v.reshape(bsz, -1, M_HEADS, M_HEAD_DIM)
    q = q.reshape(bsz, seq, M_HEADS, M_HEAD_DIM)
    s = jnp.einsum('bshd,bmhd->bhsm', q, k,
                   preferred_element_type=jnp.float32) * (M_HEAD_DIM ** -0.5)
    p = jax.nn.softmax(s, axis=-1)
    o = jnp.einsum('bhsm,bmhd->bshd', p.astype(v.dtype), v)
    return o.reshape(bsz, seq, M_WIDTH)


def setup_inputs(seed: int = 0) -> dict:
    key = jax.random.key(seed)
    ks = jax.random.split(key, 16)
    nrm = jax.random.normal
    f32 = jnp.float32
    return {
        "x": nrm(ks[0], (BATCH, SEQ, D_MODEL), f32),
        "mem": nrm(ks[1], (BATCH, MEM_LEN, D_MODEL), f32),
        "norm_g": 1.0 + 0.01 * nrm(ks[2], (DEPTH, D_MODEL), f32),
        "w_in": nrm(ks[3], (DEPTH, D_MODEL, IN_WIDTH), f32) * D_MODEL ** -0.5,
        "q_norm_g": 1.0 + 0.01 * nrm(ks[4], (DEPTH, A_HEAD_DIM), f32),
        "k_norm_g": 1.0 + 0.01 * nrm(ks[5], (DEPTH, A_HEAD_DIM), f32),
        "sg_ln_g": 1.0 + 0.01 * nrm(ks[6], (DEPTH, B_WIDTH), f32),
        "sg_ln_b": 0.01 * nrm(ks[7], (DEPTH, B_WIDTH), f32),
        "w_s": nrm(ks[8], (DEPTH, B_GROUPS, CHUNK, CHUNK), f32) * CHUNK ** -0.5,
        "b_s": 1.0 + 0.01 * nrm(ks[9], (DEPTH, B_GROUPS, CHUNK), f32),
        "mem_norm_g": 1.0 + 0.01 * nrm(ks[10], (DEPTH, D_MODEL), f32),
        "w_mem_kv": nrm(ks[11], (DEPTH, D_MODEL, 2 * M_WIDTH), f32) * D_MODEL ** -0.5,
        "w_br": nrm(ks[12], (DEPTH, N_BRANCH, BRANCH_WIDTH, D_MODEL), f32) * BRANCH_WIDTH ** -0.5,
        "w_out": nrm(ks[13], (DEPTH, D_MODEL, D_MODEL), f32) * D_MODEL ** -0.5,
        "final_g": 1.0 + 0.01 * nrm(ks[14], (D_MODEL,), f32),
    }


def reference(x, mem, norm_g, w_in, q_norm_g, k_norm_g, sg_ln_g, sg_ln_b, w_s, b_s,
              mem_norm_g, w_mem_kv, w_br, w_out, final_g):
    bsz, seq, d = x.shape
    cos, sin = axial_rope_tables(seq)
    pts = _split_points()
    for l in range(DEPTH):
        h = rms_norm(x, norm_g[l])
        proj = h @ w_in[l]
        qA, kA, vA, zA, uB, vB, zB, qM, zM, g_logits = jnp.split(proj, pts, axis=-1)
        yA = gqa_axial_attention(qA, kA, vA, q_norm_g[l], k_norm_g[l], cos, sin) * jax.nn.silu(zA)
        yB = chunked_spatial_gating(uB, vB, sg_ln_g[l], sg_ln_b[l], w_s[l], b_s[l]) * jax.nn.silu(zB)
        mem_n = rms_norm(mem, mem_norm_g[l])
        yM = memory_cross_attention(qM, mem_n, w_mem_kv[l]) * jax.nn.silu(zM)
        branches = jnp.stack([yA, yB, yM], axis=2)
        up = jnp.einsum('bsnw,nwd->bsnd', branches, w_br[l])
        gates = jax.nn.sigmoid(g_logits.reshape(bsz, seq, N_BRANCH, d))
        merged = jnp.sum(gates * up, axis=2)
        x = x + merged @ w_out[l]
    return rms_norm(x, final_g)
```

```python
import numpy as np
from contextlib import ExitStack
import concourse.bass as bass
import concourse.mybir as mybir
from concourse.bass_utils import run_bass_kernel_spmd

F32 = mybir.dt.float32
BF16 = mybir.dt.bfloat16
AF = mybir.ActivationFunctionType
ALU = mybir.AluOpType
AX = mybir.AxisListType

D = 1024
EPS = 1e-6
INW = 6912
OFF_KV = 0
OFF_G = 2048
OFF_MK = 2048 + 7 * 4096
OFF_MV = OFF_MK + 4096
OFF_MG = OFF_MV + 4096
OFF_O = OFF_MG + 8 * 4608
WTOT = OFF_O + 2 * 4096
PIECE = 2048
NPIECE = WTOT // PIECE


class Buf:
    __slots__ = ("name", "lw", "rd")

    def __init__(self, name):
        self.name = name
        self.lw = None
        self.rd = []


class Eng:
    def __init__(self, sched, name, eng):
        self.s = sched
        self.name = name
        self.eng = eng
        self.sem = None
        self.val = 0
        self.seq = 0
        self.last = None
        self.last_sig = True
        self.sigs = []
        self.known = {}

    def new_epoch(self):
        self.sem = self.s.new_sem(self.name)
        self.val = 0

    def signal_for(self, seq):
        best = None
        for sp in reversed(self.sigs):
            if sp[0] >= seq:
                best = sp
            else:
                break
        if best is not None:
            return best
        assert self.last is not None and not self.last_sig and self.seq >= seq, (self.name, seq, self.seq)
        self.val += 1
        self.last.then_inc(self.sem, 1)
        self.last_sig = True
        self.sigs.append((self.seq, self.sem, self.val))
        if len(self.sigs) > 64:
            self.sigs = self.sigs[-32:]
        return self.sigs[-1]


class Stream:
    def __init__(self, sched, name):
        self.name = name
        self.sem = sched.new_sem(name)
        self.seq = 0

    def signal_for(self, seq):
        return (self.seq, self.sem, 16 * self.seq)


class Sched:
    def __init__(self, nc, stack):
        self.nc = nc
        self.stack = stack
        self.nsem = 0
        self.pe = Eng(self, "pe", nc.tensor)
        self.act = Eng(self, "act", nc.scalar)
        self.dve = Eng(self, "dve", nc.vector)
        self.pool = Eng(self, "pool", nc.gpsimd)
        self.sp = Eng(self, "sp", nc.sync)
        self.engs = [self.pe, self.act, self.dve, self.pool, self.sp]
        for e in self.engs:
            e.new_epoch()
        self.nwaits = 0
        self.nops = 0

    def new_sem(self, name):
        self.nsem += 1
        return self.stack.enter_context(self.nc.semaphore(f"s{self.nsem}_{name}"))

    def new_epoch(self):
        for e in self.engs:
            e.new_epoch()

    def stream(self, name):
        return Stream(self, name)

    def _deps(self, E, reads, writes, strict=False):
        deps = {}

        def add(p, seq):
            if p is E and E is self.pe and not strict:
                return
            if deps.get(p, -1) < seq:
                deps[p] = seq

        for b in reads:
            if b.lw is not None:
                add(b.lw[0], b.lw[1])
        for b in writes:
            if b.lw is not None:
                add(b.lw[0], b.lw[1])
            for (p, seq) in b.rd:
                add(p, seq)
        return deps

    def _wait(self, E, deps):
        for p, seq in deps.items():
            if E.known.get(p, -1) >= seq:
                continue
            sq, sem, val = p.signal_for(seq)
            E.eng.wait_ge(sem, val)
            self.nwaits += 1
            E.known[p] = sq

    def op(self, E, fn, reads=(), writes=(), signal=False):
        self._wait(E, self._deps(E, reads, writes))
        inst = fn(E.eng)
        self.nops += 1
        E.seq += 1
        E.last = inst
        E.last_sig = False
        if signal:
            E.signal_for(E.seq)
        me = (E, E.seq)
        for b in writes:
            b.lw = me
            b.rd = []
        for b in reads:
            b.rd.append(me)
            if len(b.rd) > 40:
                d = {}
                for (p, sq) in b.rd:
                    if d.get(p, -1) < sq:
                        d[p] = sq
                b.rd = list(d.items())
        return inst

    def dma(self, Q, stream, out, in_, reads=(), writes=()):
        self._wait(Q, self._deps(Q, reads, writes, strict=True))
        inst = Q.eng.dma_start(out=out, in_=in_)
        inst.then_inc(stream.sem, 16)
        stream.seq += 1
        self.nops += 1
        me = (stream, stream.seq)
        for b in writes:
            b.lw = me
            b.rd = []
        for b in reads:
            b.rd.append(me)
        return inst

    def wait_stream(self, E, stream):
        if stream.seq > 0:
            E.eng.wait_ge(stream.sem, 16 * stream.seq)
            E.known[stream] = stream.seq


def build(NT=32, L=4, debug=False):
    assert NT % 4 == 0
    NST = NT // 4
    STOK = NT * 128
    nc = bass.Bass("TRN2", target_bir_lowering=False)

    def din(name, shape, dt=F32):
        return nc.dram_tensor(name, list(shape), dt, kind="ExternalInput").ap()

    x_d = din("x", [STOK, D])
    mem_d = din("mem", [256, D])
    wflat_d = din("wflat", [L, 128, WTOT])
    norm_g_d = din("norm_g", [L, D])
    qg_d = din("q_norm_g", [L, 64])
    kg_d = din("k_norm_g", [L, 64])
    lngT_d = din("lngT", [L, 128, 4])
    lnbT_d = din("lnbT", [L, 128, 4])
    wsT_d = din("wsT", [L, 128, 512])
    bs_d = din("b_s", [L, 512])
    memg_d = din("mem_norm_g", [L, D])
    fing_d = din("final_g", [D])
    cos_d = din("ropecos", [128, NT * 32])
    sin_d = din("ropesin", [128, NT * 32])
    ident_d = din("ident", [128, 128])
    y_d = nc.dram_tensor("y", [STOK, D], F32, kind="ExternalOutput").ap()
    xres_d = nc.dram_tensor("xres", [STOK, D], F32, kind="Internal").ap()
    wbf_d = nc.dram_tensor("wbf", [L, 128, WTOT], BF16, kind="Internal").ap()

    with ExitStack() as st:
        S = Sched(nc, st)
        PE, ACT, DVE, POOL, SP = S.pe, S.act, S.dve, S.pool, S.sp

        def sb(name, shape, dt):
            return st.enter_context(nc.sbuf_tensor("sb_" + name, list(shape), dt))

        class T:
            def __init__(self, name, shape, dt):
                self.t = sb(name, shape, dt)
                self.b = Buf(name)

        psum = st.enter_context(nc.psum_tensor("psum", [128, 4096], F32))
        pbank = [Buf(f"bank{i}") for i in range(8)]
        bank_rr = [0]

        def bank(i):
            return psum[:, i * 512:(i + 1) * 512]

        def bankb(i):
            return psum[:, i * 512:(i + 1) * 512].bitcast(BF16)

        def next_bank(allowed=(0, 1, 2, 3, 4, 5, 6, 7)):
            while True:
                bank_rr[0] = (bank_rr[0] + 1) % 8
                if bank_rr[0] in allowed:
                    return bank_rr[0]

        class Alias(T):
            def __init__(self, ap, buf):
                self.t = ap
                self.b = buf

        identf = T("identf", [128, 128], F32)
        identb = T("identb", [128, 128], BF16)
        onesb = T("onesb", [128, 128], BF16)
        mhalf = T("mhalf", [128, 32], F32)
        cosT = T("cosT", [128, NT * 32], F32)
        sinT = T("sinT", [128, NT * 32], F32)
        gN = T("gN", [128, D], F32)
        qg = T("qg", [128, 64], F32)
        kg = T("kg", [128, 64], F32)
        lngT = T("lngT", [128, 4], F32)
        lnbT = T("lnbT", [128, 4], F32)
        biasT = T("biasT", [128, 512], F32)
        wsT = T("wsT", [128, 512], BF16)
        memss = T("memss", [128, 2], F32)
        memrs = T("memrs", [128, 2], F32)
        KmT = T("KmT", [128, 4, 256], BF16)
        Vm = T("Vm", [128, 2, 512], BF16)
        KT = T("KT", [128, STOK], BF16)
        Vaug = T("Vaug", [128, NT, 2, 128], BF16)
        ssq = [T("ssqA", [128, NT], F32), T("ssqB", [128, NT], F32)]
        rstd = T("rstd", [128, NT], F32)
        junk = T("junk", [128, D], BF16)
        xin = [T(f"xin{i}", [128, D], F32) for i in range(2)]
        hb = [T(f"hb{i}", [128, D], BF16) for i in range(2)]
        hT = [T(f"hT{i}", [128, 8, 512], BF16) for i in range(2)]
        wsl = [T(f"wsl{i}", [128, 4096], BF16) for i in range(3)]
        wms = [T(f"wms{i}", [128, 4608], BF16) for i in range(2)]
        fA = T("fA", [128, 512], F32)
        fB = T("fB", [128, 512], F32)
        fC = T("fC", [128, 512], F32)
        fD = T("fD", [128, 512], F32)
        sm8 = T("sm8", [128, 8], F32)
        sm8b = T("sm8b", [128, 8], F32)
        ssq_q = T("ssq_q", [128, 32], F32)
        rq = T("rq", [128, 32], F32)
        st6 = T("st6", [128, 4, 6], F32)
        mv4 = T("mv4", [128, 4, 2], F32)
        rs4 = T("rs4", [128, 4], F32)
        nb4 = T("nb4", [128, 4], F32)
        QT = T("QT", [128, 4, 512], BF16)
        zAs = T("zAs", [128, 4, 512], BF16)
        uT = T("uT", [128, 4, 512], BF16)
        zBs = T("zBs", [128, 4, 512], BF16)
        qMT = T("qMT", [128, 4, 512], BF16)
        zMs = T("zMs", [128, 4, 512], BF16)
        vnb = T("vnb", [128, 4, 512], BF16)
        gA = T("gA", [128, 4, 512], BF16)
        gB = T("gB", [128, 4, 512], BF16)
        gMb = T("gMb", [128, 4, 512], BF16)
        PTt = sb("PTt", [128, 3072], BF16)
        PT = [Alias(PTt[:, i * 1024:(i + 1) * 1024], Buf(f"PT{i}")) for i in range(3)]
        PmT = [T(f"PmT{i}", [128, 512], BF16) for i in range(2)]
        tg = [T(f"tg{i}", [128, 512], F32) for i in range(3)]
        mergedT = T("mergedT", [128, 8, 512], BF16)
        xr = [T(f"xr{i}", [128, D], F32) for i in range(2)]
        memT = Alias(zAs.t[:].rearrange("p a (b c) -> p (a b) c", b=2), zAs.b)
        qf4 = Alias(mergedT.t[:].rearrange("p k t -> p (k t)").bitcast(F32), mergedT.b)
        qrotk = [Alias(PTt[:, 0:512], PT[0].b), Alias(PTt[:, 1024:1536], PT[1].b)]
        PIECE2 = 512
        NP2 = WTOT // PIECE2
        sf = [T(f"sf{i}", [128, PIECE2], F32) for i in range(2)]
        sbf = [T(f"sbf{i}", [128, PIECE2], BF16) for i in range(2)]
        wbfB = [[Buf(f"wbf{l}_{i}") for i in range(2)] for l in range(L)]
        stg_f_ap = [wms[i].t[:].bitcast(F32)[:, 0:PIECE] for i in range(2)]
        stg_b_ap = [wsl[i].t[:, 0:PIECE] for i in range(2)]

        st_setup = S.stream("setup")
        st_xin = [S.stream(f"xin{i}") for i in range(2)]
        st_xr = [S.stream(f"xr{i}") for i in range(2)]
        st_wsl = [S.stream(f"wsl{i}") for i in range(3)]
        st_wms = [S.stream(f"wms{i}") for i in range(2)]
        st_stgl = [S.stream(f"stgl{i}") for i in range(2)]
        st_stgs = [S.stream(f"stgs{i}") for i in range(2)]
        st_sfl = [S.stream(f"sfl{i}") for i in range(2)]
        st_sfs = [S.stream(f"sfs{i}") for i in range(2)]
        st_xst = [S.stream(f"xst{i}") for i in range(2)]
        st_tab = S.stream("tab")
        st_y = S.stream("y")

        def OP(E, fn, r=(), w=(), sig=False):
            return S.op(E, fn, reads=[t.b if isinstance(t, T) else t for t in r],
                        writes=[t.b if isinstance(t, T) else t for t in w], signal=sig)

        def DMA(stream, out, in_, r=(), w=()):
            return S.dma(SP, stream, out, in_, reads=[t.b if isinstance(t, T) else t for t in r],
                         writes=[t.b if isinstance(t, T) else t for t in w])

        DMA(st_setup, identf.t[:], ident_d, w=[identf])
        DMA(st_setup, cosT.t[:], cos_d, w=[cosT])
        DMA(st_setup, sinT.t[:], sin_d, w=[sinT])
        OP(DVE, lambda e: e.tensor_copy(out=identb.t[:], in_=identf.t[:]), r=[identf], w=[identb])
        OP(POOL, lambda e: e.memset(mhalf.t[:], -0.5), w=[mhalf])
        OP(POOL, lambda e: e.memset(onesb.t[:], 1.0), w=[onesb])
        OP(POOL, lambda e: e.memset(Vaug.t[:, :, 0, 64:128], 1.0), w=[Vaug])
        OP(POOL, lambda e: e.memset(Vaug.t[:, :, 1, 0:64], 1.0), w=[Vaug])
        xin_ctr = [0]

        def load_tile(src_ap):
            s = xin_ctr[0] % 2
            xin_ctr[0] += 1
            DMA(st_xin[s], xin[s].t[:], src_ap, w=[xin[s]])
            return xin[s]

        for t in range(2):
            mt = load_tile(mem_d[t * 128:(t + 1) * 128, :])
            OP(ACT, lambda e: e.activation(out=junk.t[:], in_=mt.t[:], func=AF.Square,
                                           accum_out=memss.t[:, t:t + 1]), r=[mt], w=[junk, memss])
        OP(DVE, lambda e: e.tensor_scalar(out=memrs.t[:], in0=memss.t[:], scalar1=1.0 / D, scalar2=EPS,
                                          op0=ALU.mult, op1=ALU.add), r=[memss], w=[memrs])
        OP(POOL, lambda e: e.tensor_tensor(out=memrs.t[:], in0=memrs.t[:], in1=mhalf.t[:, 0:2], op=ALU.pow),
           r=[memrs, mhalf], w=[memrs])

        cast_engs = [POOL, DVE, ACT]
        npc = 0
        up_pieces = [OFF_KV // PIECE] + list(range(OFF_MK // PIECE, OFF_MG // PIECE))
        for l in range(1):
            for pi in up_pieces:
                s = npc % 2
                ce = cast_engs[npc % 3]
                DMA(st_stgl[s], stg_f_ap[s], wflat_d[l, :, pi * PIECE:(pi + 1) * PIECE], w=[wms[s]])
                if ce is ACT:
                    OP(ACT, lambda e: e.activation(out=stg_b_ap[s], in_=stg_f_ap[s], func=AF.Copy),
                       r=[wms[s]], w=[wsl[s]])
                else:
                    OP(ce, lambda e: e.tensor_copy(out=stg_b_ap[s], in_=stg_f_ap[s]), r=[wms[s]], w=[wsl[s]])
                DMA(st_stgs[s], wbf_d[l, :, pi * PIECE:(pi + 1) * PIECE], stg_b_ap[s], r=[wsl[s]])
                npc += 1
        for s in range(2):
            S.wait_stream(SP, st_stgs[s])

        def x_src(l):
            return x_d if l == 0 else xres_d

        def load_x(l, tile):
            return load_tile(x_src(l)[tile * 128:(tile + 1) * 128, :])


        wuse = {"n": 0, "loaded": 0, "plan": [], "free": [True, True, True]}
        muse = {"n": 0, "loaded": 0, "plan": [], "free": [True, True]}
        for l in range(L):
            wuse["plan"] += [(l, OFF_MK, 4096), (l, OFF_MV, 4096), (l, OFF_KV, 2048)]
            for s_ in range(NST):
                wuse["plan"] += [(l, OFF_G + g * 4096, 4096) for g in (0, 1, 2, 3, 4, 6, 5)]
                wuse["plan"] += [(l, OFF_O, 4096), (l, OFF_O + 4096, 4096)]
                muse["plan"] += [(l, OFF_MG + cb * 4608, 4608) for cb in range(8)]

        def w_prefetch(upto):
            while wuse["loaded"] <= min(upto, len(wuse["plan"]) - 1):
                u = wuse["loaded"]
                s = u % 3
                assert wuse["free"][s], ("w slot not released", u)
                l, off, n = wuse["plan"][u]
                stage_need(l, off, n)
                DMA(st_wsl[s], wsl[s].t[:, 0:n], wbf_d[l, :, off:off + n], r=wbfB[l], w=[wsl[s]])
                wuse["free"][s] = False
                wuse["loaded"] += 1

        def w_get():
            u = wuse["n"]
            assert u < wuse["loaded"]
            return u, wsl[u % 3]

        def w_release(u):
            wuse["free"][u % 3] = True
            if u == wuse["n"]:
                wuse["n"] += 1

        def m_prefetch(upto):
            while muse["loaded"] <= min(upto, len(muse["plan"]) - 1):
                u = muse["loaded"]
                s = u % 2
                assert muse["free"][s]
                l, off, n = muse["plan"][u]
                stage_need(l, off, n)
                DMA(st_wms[s], wms[s].t[:, 0:n], wbf_d[l, :, off:off + n], r=wbfB[l], w=[wms[s]])
                muse["free"][s] = False
                muse["loaded"] += 1

        stg = {"l": None, "i": 0, "pieces": []}
        stg_done = [set() for _ in range(L)]

        def stage_need(l, off, n):
            need = range(off // PIECE2, (off + n + PIECE2 - 1) // PIECE2)
            while any(k not in stg_done[l] for k in need):
                assert stg["l"] == l, ("staging for layer not active", l, stg["l"])
                stage_step()

        def stage_begin(l, pieces=None):
            stg["l"] = l
            stg["i"] = 0
            stg["pieces"] = list(range(NP2)) if pieces is None else list(pieces)

        def stage_step(use_dve=False):
            l = stg["l"]
            if l is None:
                return
            i = stg["i"]
            pcs = stg["pieces"]
            n = len(pcs)
            if i >= n + 2:
                stg["l"] = None
                return
            if i < n:
                DMA(st_sfl[i % 2], sf[i % 2].t[:], wflat_d[l, :, pcs[i] * PIECE2:(pcs[i] + 1) * PIECE2], w=[sf[i % 2]])
            j = i - 1
            if 0 <= j < n:
                ce = DVE if (use_dve and j % 2 == 1) else POOL
                OP(ce, lambda e: e.tensor_copy(out=sbf[j % 2].t[:], in_=sf[j % 2].t[:]), r=[sf[j % 2]], w=[sbf[j % 2]])
            k = i - 2
            if 0 <= k < n:
                S.dma(SP, st_sfs[k % 2], wbf_d[l, :, pcs[k] * PIECE2:(pcs[k] + 1) * PIECE2], sbf[k % 2].t[:],
                      reads=[sbf[k % 2].b], writes=[wbfB[l][k % 2]])
                stg_done[l].add(pcs[k])
            stg["i"] += 1

        def stage_flush():
            while stg["l"] is not None:
                stage_step()

        up_set = set()
        for pi in up_pieces:
            up_set.update(range(pi * (PIECE // PIECE2), (pi + 1) * (PIECE // PIECE2)))
        stg_done[0].update(up_set)
        stage_begin(0, [k for k in range(NP2) if k not in up_set])
        pend = load_x(0, 0)
        for i in range(NT):
            cur = pend
            if i + 1 < NT:
                pend = load_x(0, i + 1)
            OP(ACT, lambda e: e.activation(out=junk.t[:], in_=cur.t[:], func=AF.Square,
                                           accum_out=ssq[0].t[:, i:i + 1]), r=[cur], w=[junk, ssq[0]])
            stage_step(True)
            stage_step(True)
            stage_step(True)

        hb_ctr = [0]

        def hT_part1(xt_ap, xt_T, gtab, rs_ap, rs_T):
            s = hb_ctr[0] % 2
            hb_ctr[0] += 1
            h = hb[s]
            OP(DVE, lambda e: e.scalar_tensor_tensor(out=h.t[:], in0=xt_ap, scalar=rs_ap, in1=gtab.t[:],
                                                     op0=ALU.mult, op1=ALU.mult), r=[gtab, rs_T, xt_T], w=[h])
            return h

        def hT_part2(h, dstT, col0, banks=(0, 1, 2, 3, 4, 5, 6, 7)):
            b = next_bank(banks)
            for k in range(8):
                OP(PE, lambda e: e.transpose(out=bankb(b)[:, k * 128:(k + 1) * 128], in_=h.t[:, k * 128:(k + 1) * 128],
                                             identity=identb.t[:]), r=[h, identb], w=[pbank[b]], sig=(k == 7))
            OP(DVE, lambda e: e.tensor_copy(out=dstT.t[:, :, col0:col0 + 128],
                                            in_=bankb(b).rearrange("p (k t) -> p k t", k=8)), r=[pbank[b]], w=[dstT])

        def make_hT(xt_ap, xt_T, gtab, rs_ap, rs_T, dstT, col0, banks=(0, 1, 2, 3, 4, 5, 6, 7)):
            hT_part2(hT_part1(xt_ap, xt_T, gtab, rs_ap, rs_T), dstT, col0, banks)

        def q3(ap):
            return ap.rearrange("p (h d) -> p h d", h=8)

        def norm8(src, gtab):
            OP(DVE, lambda e: e.tensor_tensor(out=fA.t[:], in0=src.t[:], in1=src.t[:], op=ALU.mult), r=[src], w=[fA])
            OP(DVE, lambda e: e.tensor_reduce(out=sm8.t[:], in_=q3(fA.t[:]), axis=AX.X, op=ALU.add), r=[fA], w=[sm8])
            OP(DVE, lambda e: e.tensor_scalar(out=sm8b.t[:], in0=sm8.t[:], scalar1=1.0 / 64, scalar2=EPS,
                                              op0=ALU.mult, op1=ALU.add), r=[sm8], w=[sm8b])
            OP(POOL, lambda e: e.tensor_tensor(out=sm8b.t[:], in0=sm8b.t[:], in1=mhalf.t[:, 0:8], op=ALU.pow),
               r=[sm8b, mhalf], w=[sm8b])
            OP(DVE, lambda e: e.tensor_tensor(out=q3(src.t[:]), in0=q3(src.t[:]),
                                              in1=sm8b.t[:].unsqueeze(2).broadcast_to([128, 8, 64]), op=ALU.mult),
               r=[src, sm8b], w=[src])
            OP(DVE, lambda e: e.tensor_tensor(out=q3(src.t[:]), in0=q3(src.t[:]),
                                              in1=gtab.t[:].unsqueeze(1).broadcast_to([128, 8, 64]), op=ALU.mult),
               r=[src, gtab], w=[src])

        def rope_q4(tile0, U=4, H=8):
            n = U * H * 64
            m = U * H * 16

            def xv(ap, sg, r):
                return ap.rearrange("p (u h s r f) -> p u h s r f", u=U, h=H, s=2, r=2)[:, :, :, sg, r, :]

            def tv(t):
                return t.t[:, 0:m].rearrange("p (u h f) -> p u h f", u=U, h=H)

            def tb(t, sg):
                return t.t[:, tile0 * 32:(tile0 + U) * 32].rearrange("p (u s f) -> p u s f", u=U, s=2)[:, :, sg, :] \
                    .unsqueeze(2).broadcast_to([128, U, H, 16])
            src = qf4.t[:, 0:n]
            dst = PTt[:, 0:n]
            PTs = [PT[0], PT[1]]
            for sg in range(2):
                OP(DVE, lambda e: e.tensor_tensor(out=tv(fA), in0=xv(src, sg, 0), in1=tb(cosT, sg), op=ALU.mult), r=[qf4, cosT], w=[fA])
                OP(DVE, lambda e: e.tensor_tensor(out=tv(fC), in0=xv(src, sg, 1), in1=tb(sinT, sg), op=ALU.mult), r=[qf4, sinT], w=[fC])
                OP(DVE, lambda e: e.tensor_tensor(out=xv(dst, sg, 0), in0=tv(fA), in1=tv(fC), op=ALU.subtract), r=[fA, fC], w=PTs)
                OP(DVE, lambda e: e.tensor_tensor(out=tv(fA), in0=xv(src, sg, 1), in1=tb(cosT, sg), op=ALU.mult), r=[qf4, cosT], w=[fA])
                OP(DVE, lambda e: e.tensor_tensor(out=tv(fC), in0=xv(src, sg, 0), in1=tb(sinT, sg), op=ALU.mult), r=[qf4, sinT], w=[fC])
                OP(DVE, lambda e: e.tensor_tensor(out=xv(dst, sg, 1), in0=tv(fA), in1=tv(fC), op=ALU.add), r=[fA, fC], w=PTs)

        def rope_k(src, dst, tile0):
            def v5(t, hd, r):
                return t.t[:].rearrange("p (u h s r f) -> p u h s r f", u=4, h=2, s=2, r=2)[:, :, hd, :, r, :]

            def v4(t, hd):
                return t.t[:, 0:256].rearrange("p (u h s f) -> p u h s f", u=4, h=2, s=2)[:, :, hd, :, :]

            def tb(t):
                return t.t[:, tile0 * 32:(tile0 + 4) * 32].rearrange("p (u s f) -> p u s f", u=4, s=2)
            for hd in range(2):
                OP(DVE, lambda e: e.tensor_tensor(out=v4(fA, hd), in0=v5(src, hd, 0), in1=tb(cosT), op=ALU.mult), r=[src, cosT], w=[fA])
                OP(DVE, lambda e: e.tensor_tensor(out=v4(fC, hd), in0=v5(src, hd, 1), in1=tb(sinT), op=ALU.mult), r=[src, sinT], w=[fC])
                OP(DVE, lambda e: e.tensor_tensor(out=v5(dst, hd, 0), in0=v4(fA, hd), in1=v4(fC, hd), op=ALU.subtract), r=[fA, fC], w=[dst])
                OP(DVE, lambda e: e.tensor_tensor(out=v4(fA, hd), in0=v5(src, hd, 1), in1=tb(cosT), op=ALU.mult), r=[src, cosT], w=[fA])
                OP(DVE, lambda e: e.tensor_tensor(out=v4(fC, hd), in0=v5(src, hd, 0), in1=tb(sinT), op=ALU.mult), r=[src, sinT], w=[fC])
                OP(DVE, lambda e: e.tensor_tensor(out=v5(dst, hd, 1), in0=v4(fA, hd), in1=v4(fC, hd), op=ALU.add), r=[fA, fC], w=[dst])

        def silu2_evac(b, dst_ap, dstT):
            OP(ACT, lambda e: e.activation(out=dst_ap, in_=bank(b), func=AF.Silu), r=[pbank[b]], w=[dstT])

        def emit_hT_sub(l, tile, dstT, sub, curx):
            make_hT(curx.t[:], curx, gN, rstd.t[:, tile:tile + 1], rstd, dstT, sub * 128)

        cur = 0
        for l in range(L):
            if l > 0:
                S.new_epoch()
            ssq_cur, ssq_nxt = ssq[cur], ssq[1 - cur]
            gM = xr[0]
            DMA(st_tab, gN.t[:], norm_g_d[l].partition_broadcast(128), w=[gN])
            DMA(st_tab, gM.t[:], memg_d[l].partition_broadcast(128), w=[gM])
            DMA(st_tab, qg.t[:], qg_d[l].partition_broadcast(128), w=[qg])
            DMA(st_tab, kg.t[:], kg_d[l].partition_broadcast(128), w=[kg])
            DMA(st_tab, lngT.t[:], lngT_d[l], w=[lngT])
            DMA(st_tab, lnbT.t[:], lnbT_d[l], w=[lnbT])
            DMA(st_tab, biasT.t[:], bs_d[l].partition_broadcast(128), w=[biasT])
            DMA(st_tab, fD.t[:], wsT_d[l], w=[fD])
            OP(DVE, lambda e: e.tensor_copy(out=wsT.t[:], in_=fD.t[:]), r=[fD], w=[wsT])
            OP(DVE, lambda e: e.tensor_scalar(out=rstd.t[:], in0=ssq_cur.t[:], scalar1=1.0 / D, scalar2=EPS,
                                              op0=ALU.mult, op1=ALU.add), r=[ssq_cur], w=[rstd])
            OP(POOL, lambda e: e.tensor_tensor(out=rstd.t[:], in0=rstd.t[:], in1=mhalf.t[:, 0:NT], op=ALU.pow),
               r=[rstd, mhalf], w=[rstd])
            b = next_bank()
            OP(PE, lambda e: e.matmul(bank(b), lhsT=onesb.t[:], rhs=wsT.t[:], start=True, stop=True), r=[onesb, wsT], w=[pbank[b]], sig=True)
            for g in range(4):
                OP(DVE, lambda e: e.scalar_tensor_tensor(out=biasT.t[:, g * 128:(g + 1) * 128], in0=bank(b)[:, g * 128:(g + 1) * 128],
                                                         scalar=lnbT.t[:, g:g + 1], in1=biasT.t[:, g * 128:(g + 1) * 128],
                                                         op0=ALU.mult, op1=ALU.add), r=[pbank[b], lnbT, biasT], w=[biasT])

            w_prefetch(wuse["n"] + 2)
            for t in range(2):
                mt = load_tile(mem_d[t * 128:(t + 1) * 128, :])
                make_hT(mt.t[:], mt, gM, memrs.t[:, t:t + 1], memrs, memT, t * 128)
            u, ws = w_get()
            wv = ws.t[:].rearrange("p (k c) -> p k c", k=8)
            for h in range(4):
                b = next_bank()
                for k in range(8):
                    OP(PE, lambda e: e.matmul(bank(b)[:, 0:256], lhsT=wv[:, k, h * 128:(h + 1) * 128], rhs=memT.t[:, k, :],
                                              start=(k == 0), stop=(k == 7)), r=[ws, memT], w=[pbank[b]], sig=(k == 7))
                OP(DVE, lambda e: e.tensor_copy(out=KmT.t[:, h, :], in_=bank(b)[:, 0:256]), r=[pbank[b]], w=[KmT])
            w_release(u)
            w_prefetch(wuse["n"] + 2)
            u, ws = w_get()
            wv = ws.t[:].rearrange("p (k c) -> p k c", k=8)
            for t in range(2):
                b = next_bank()
                for k in range(8):
                    OP(PE, lambda e: e.matmul(bank(b), lhsT=memT.t[:, k, t * 128:(t + 1) * 128], rhs=wv[:, k, :],
                                              start=(k == 0), stop=(k == 7)), r=[ws, memT], w=[pbank[b]], sig=(k == 7))
                OP(DVE, lambda e: e.tensor_copy(out=Vm.t[:, t, :], in_=bank(b)), r=[pbank[b]], w=[Vm])
            w_release(u)
            w_prefetch(wuse["n"] + 2)

            u, ws = w_get()
            wkv = ws.t[:, 0:2048].rearrange("p (k c) -> p k c", k=8)
            pend = load_x(l, 0)
            hq = [None]
            NBT = min(16, NT)
            for bt in range(NT // NBT):
                for sb in range(NBT // 4):
                    s_ = bt * (NBT // 4) + sb
                    hTc = hT[s_ % 2]
                    for sub in range(4):
                        tile = s_ * 4 + sub
                        if hq[0] is None:
                            curx = pend
                            if tile + 1 < NT:
                                pend = load_x(l, tile + 1)
                            hq[0] = hT_part1(curx.t[:], curx, gN, rstd.t[:, tile:tile + 1], rstd)
                        hcur = hq[0]
                        hq[0] = None
                        if tile + 1 < NT:
                            curx = pend
                            if tile + 2 < NT:
                                pend = load_x(l, tile + 2)
                            hq[0] = hT_part1(curx.t[:], curx, gN, rstd.t[:, tile + 1:tile + 2], rstd)
                        hT_part2(hcur, hTc, sub * 128, banks=(2, 3, 4, 5, 6, 7))
                        stage_step()
                        stage_step()
                        stage_step()
                    for sub in range(4):
                        bb = sub // 2
                        o = bank(bb)[:, (sub % 2) * 256:(sub % 2) * 256 + 256]
                        for k in range(8):
                            OP(PE, lambda e: e.matmul(o, lhsT=hTc.t[:, k, sub * 128:(sub + 1) * 128], rhs=wkv[:, k, :],
                                                      start=(k == 0), stop=(k == 7)), r=[ws, hTc], w=[pbank[bb]], sig=(k == 7))
                    pkv = psum[:, 0:1024].rearrange("p (u c) -> p u c", u=4)
                    OP(ACT, lambda e: e.activation(out=Vaug.t[:, s_ * 4:s_ * 4 + 4, 0, 0:64], in_=pkv[:, :, 128:192], func=AF.Copy),
                       r=[pbank[0], pbank[1]], w=[Vaug])
                    OP(ACT, lambda e: e.activation(out=Vaug.t[:, s_ * 4:s_ * 4 + 4, 1, 64:128], in_=pkv[:, :, 192:256], func=AF.Copy),
                       r=[pbank[0], pbank[1]], w=[Vaug])
                    OP(ACT, lambda e: e.activation(out=qf4.t[:, sb * 512:(sb + 1) * 512].rearrange("p (u c) -> p u c", u=4),
                                                   in_=pkv[:, :, 0:128], func=AF.Copy), r=[pbank[0], pbank[1]], w=[qf4])
                    for sub in range(4):
                        for hd in range(2):
                            ci = (sb * 4 + sub) * 2 + hd
                            OP(ACT, lambda e: e.activation(out=junk.t[:, 0:64], in_=pkv[:, sub, hd * 64:(hd + 1) * 64], func=AF.Square,
                                                           accum_out=ssq_q.t[:, ci:ci + 1]), r=[pbank[0], pbank[1]], w=[junk, ssq_q])
                nh = NBT * 2
                OP(DVE, lambda e: e.tensor_scalar(out=rq.t[:, 0:nh], in0=ssq_q.t[:, 0:nh], scalar1=1.0 / 64, scalar2=EPS,
                                                  op0=ALU.mult, op1=ALU.add), r=[ssq_q], w=[rq])
                OP(POOL, lambda e: e.tensor_tensor(out=rq.t[:, 0:nh], in0=rq.t[:, 0:nh], in1=mhalf.t[:, 0:nh], op=ALU.pow), r=[rq, mhalf], w=[rq])
                kfv = qf4.t[:, 0:nh * 64].rearrange("p (h d) -> p h d", h=nh)
                OP(DVE, lambda e: e.tensor_tensor(out=kfv, in0=kfv, in1=rq.t[:, 0:nh].unsqueeze(2).broadcast_to([128, nh, 64]), op=ALU.mult),
                   r=[qf4, rq], w=[qf4])
                OP(DVE, lambda e: e.tensor_tensor(out=kfv, in0=kfv, in1=kg.t[:].unsqueeze(1).broadcast_to([128, nh, 64]), op=ALU.mult),
                   r=[qf4, kg], w=[qf4])
                rope_q4(bt * NBT, U=NBT, H=2)
                for hb_ in range((NBT + 7) // 8):
                    nt_ = min(8, NBT - hb_ * 8)
                    b = next_bank((2, 3, 4, 5, 6, 7))
                    for t_ in range(nt_):
                        idx = hb_ * 8 + t_
                        OP(PE, lambda e: e.transpose(out=bankb(b)[:, t_ * 128:(t_ + 1) * 128], in_=PTt[:, idx * 128:(idx + 1) * 128],
                                                     identity=identb.t[:]), r=[PT[0], PT[1], identb], w=[pbank[b]], sig=(t_ == nt_ - 1))
                    c0 = (bt * NBT + hb_ * 8) * 128
                    OP(DVE, lambda e: e.tensor_copy(out=KT.t[:, c0:c0 + nt_ * 128], in_=bankb(b)[:, 0:nt_ * 128]), r=[pbank[b]], w=[KT])
            w_release(u)

            xr_ctr = [0]
            stage_flush()
            if l + 1 < L:
                stage_begin(l + 1)
            pend = load_x(l, 0)
            for sub in range(4):
                curx = pend
                if sub + 1 < 4:
                    pend = load_x(l, sub + 1)
                emit_hT_sub(l, sub, hT[0], sub, curx)
            for s_ in range(NST):
                hTc = hT[s_ % 2]
                w_prefetch(wuse["n"] + 1)
                m_prefetch(muse["n"] + 1)

                def fm_block(ws, wv4, blk):
                    b = next_bank()
                    for k in range(8):
                        OP(PE, lambda e: e.matmul(bank(b), lhsT=wv4[:, blk, k, :], rhs=hTc.t[:, k, :],
                                                  start=(k == 0), stop=(k == 7)), r=[ws, hTc], w=[pbank[b]], sig=(k == 7))
                    return b

                def tm_sub(ws, wv4, sub):
                    b = next_bank()
                    for k in range(8):
                        OP(PE, lambda e: e.matmul(bank(b), lhsT=hTc.t[:, k, sub * 128:(sub + 1) * 128], rhs=wv4[:, :, k, :],
                                                  start=(k == 0), stop=(k == 7)), r=[ws, hTc], w=[pbank[b]], sig=(k == 7))
                    return b

                def wview(ws):
                    return ws.t[:].rearrange("p (b k c) -> p b k c", b=4, k=8)

                u, ws = w_get()
                w_prefetch(u + 2)
                qb = []
                for sub in range(4):
                    b = tm_sub(ws, wview(ws), sub)
                    qb.append(b)
                    for hh in range(8):
                        OP(ACT, lambda e: e.activation(out=junk.t[:, 0:64], in_=bank(b)[:, hh * 64:(hh + 1) * 64], func=AF.Square,
                                                       accum_out=ssq_q.t[:, sub * 8 + hh:sub * 8 + hh + 1]),
                           r=[pbank[b]], w=[junk, ssq_q])
                w_release(u)
                OP(DVE, lambda e: e.tensor_scalar(out=rq.t[:], in0=ssq_q.t[:], scalar1=1.0 / 64, scalar2=EPS,
                                                  op0=ALU.mult, op1=ALU.add), r=[ssq_q], w=[rq])
                OP(POOL, lambda e: e.tensor_tensor(out=rq.t[:], in0=rq.t[:], in1=mhalf.t[:, 0:32], op=ALU.pow), r=[rq, mhalf], w=[rq])
                u, ws = w_get()
                w_prefetch(u + 2)
                for blk in range(4):
                    b = fm_block(ws, wview(ws), blk)
                    silu2_evac(b, zAs.t[:, blk, :], zAs)
                w_release(u)
                for sub in range(4):
                    OP(DVE, lambda e: e.tensor_tensor(out=q3(qf4.t[:, sub * 512:(sub + 1) * 512]), in0=q3(bank(qb[sub])),
                                                      in1=rq.t[:, sub * 8:(sub + 1) * 8].unsqueeze(2).broadcast_to([128, 8, 64]),
                                                      op=ALU.mult), r=[pbank[qb[sub]], rq], w=[qf4])
                OP(DVE, lambda e: e.tensor_tensor(out=qf4.t[:].rearrange("p (h d) -> p h d", h=32),
                                                  in0=qf4.t[:].rearrange("p (h d) -> p h d", h=32),
                                                  in1=qg.t[:].unsqueeze(1).broadcast_to([128, 32, 64]), op=ALU.mult), r=[qf4, qg], w=[qf4])
                rope_q4(s_ * 4)
                u, ws = w_get()
                w_prefetch(u + 2)
                for blk in range(4):
                    b = fm_block(ws, wview(ws), blk)
                    OP(ACT, lambda e: e.activation(out=uT.t[:, blk, :], in_=bank(b), func=AF.Copy), r=[pbank[b]], w=[uT])
                w_release(u)
                u, ws = w_get()
                w_prefetch(u + 2)
                vb = []
                for sub in range(4):
                    b = tm_sub(ws, wview(ws), sub)
                    vb.append(b)
                    OP(DVE, lambda e: e.bn_stats(out=st6.t[:, sub, :], in_=bank(b)), r=[pbank[b]], w=[st6])
                    OP(DVE, lambda e: e.bn_aggr(out=mv4.t[:, sub, :], in_=st6.t[:, sub, :]), r=[st6], w=[mv4])
                w_release(u)
                OP(DVE, lambda e: e.tensor_scalar(out=rs4.t[:], in0=mv4.t[:, :, 1], scalar1=EPS, scalar2=None, op0=ALU.add),
                   r=[mv4], w=[rs4])
                OP(POOL, lambda e: e.tensor_tensor(out=rs4.t[:], in0=rs4.t[:], in1=mhalf.t[:, 0:4], op=ALU.pow), r=[rs4, mhalf], w=[rs4])
                OP(DVE, lambda e: e.scalar_tensor_tensor(out=nb4.t[:], in0=mv4.t[:, :, 0], scalar=-1.0, in1=rs4.t[:],
                                                         op0=ALU.mult, op1=ALU.mult), r=[mv4, rs4], w=[nb4])
                for sub in range(4):
                    OP(ACT, lambda e: e.activation(out=vnb.t[:, sub, :], in_=bank(vb[sub]), func=AF.Identity,
                                                   scale=rs4.t[:, sub:sub + 1], bias=nb4.t[:, sub:sub + 1]),
                       r=[pbank[vb[sub]], nb4, rs4], w=[vnb])
                u, ws = w_get()
                w_prefetch(u + 2)
                for blk in range(4):
                    b = fm_block(ws, wview(ws), blk)
                    silu2_evac(b, zBs.t[:, blk, :], zBs)
                w_release(u)
                for g in range(4):
                    OP(POOL, lambda e: e.tensor_tensor(out=uT.t[:, g, :], in0=uT.t[:, g, :], in1=zBs.t[:, g, :], op=ALU.mult),
                       r=[uT, zBs], w=[uT])
                u, ws = w_get()
                w_prefetch(u + 2)
                for blk in range(4):
                    b = fm_block(ws, wview(ws), blk)
                    silu2_evac(b, zMs.t[:, blk, :], zMs)
                w_release(u)

                for half in range(2):
                    b2 = next_bank()
                    for sj in range(8):
                        idx = half * 8 + sj
                        OP(PE, lambda e: e.transpose(out=bankb(b2)[:, sj * 128:(sj + 1) * 128], in_=PTt[:, idx * 128:(idx + 1) * 128],
                                                     identity=identb.t[:]), r=[PT[0], PT[1], identb], w=[pbank[b2]], sig=(sj == 7))
                    src = bankb(b2).rearrange("p (u j t) -> p u j t", u=2, j=4)
                    dstv = QT.t[:, :, half * 256:(half + 1) * 256].rearrange("p j (u t) -> p u j t", u=2)
                    OP(DVE, lambda e: e.tensor_copy(out=dstv, in_=src), r=[pbank[b2]], w=[QT])

                u, ws = w_get()
                w_prefetch(u + 2)

                def g5(blk):
                    b = fm_block(ws, wview(ws), blk)
                    OP(ACT, lambda e: e.activation(out=qMT.t[:, blk, :], in_=bank(b), func=AF.Copy), r=[pbank[b]], w=[qMT])

                def m_qk(h):
                    for kt in range(2):
                        b = next_bank()
                        OP(PE, lambda e: e.matmul(bank(b), lhsT=KmT.t[:, h, kt * 128:(kt + 1) * 128], rhs=qMT.t[:, h, :],
                                                  start=True, stop=True), r=[KmT, qMT], w=[pbank[b]], sig=True)
                        OP(ACT, lambda e: e.activation(out=PmT[kt].t[:], in_=bank(b), func=AF.Exp, scale=128 ** -0.5),
                           r=[pbank[b]], w=[PmT[kt]], sig=True)

                def m_pv(h):
                    bo = next_bank()
                    bl = next_bank()
                    for kt in range(2):
                        OP(PE, lambda e: e.matmul(bank(bo), lhsT=Vm.t[:, kt, h * 128:(h + 1) * 128], rhs=PmT[kt].t[:],
                                                  start=(kt == 0), stop=(kt == 1)), r=[Vm, PmT[kt]], w=[pbank[bo]])
                        OP(PE, lambda e: e.matmul(bank(bl), lhsT=onesb.t[:], rhs=PmT[kt].t[:],
                                                  start=(kt == 0), stop=(kt == 1)), r=[onesb, PmT[kt]], w=[pbank[bl]], sig=(kt == 1))
                    tm = tg[h % 2]
                    OP(DVE, lambda e: e.reciprocal(out=tm.t[:], in_=bank(bl)), r=[pbank[bl]], w=[tm])
                    OP(DVE, lambda e: e.tensor_tensor(out=tm.t[:], in0=bank(bo), in1=tm.t[:], op=ALU.mult), r=[pbank[bo], tm], w=[tm])
                    OP(POOL, lambda e: e.tensor_tensor(out=gMb.t[:, h, :], in0=tm.t[:], in1=zMs.t[:, h, :], op=ALU.mult),
                       r=[tm, zMs], w=[gMb])

                def spatial(g):
                    b = next_bank()
                    for sub in range(4):
                        OP(PE, lambda e: e.matmul(bank(b)[:, sub * 128:(sub + 1) * 128], lhsT=vnb.t[:, sub, g * 128:(g + 1) * 128],
                                                  rhs=wsT.t[:, g * 128:(g + 1) * 128], start=True, stop=True),
                           r=[vnb, wsT], w=[pbank[b]], sig=(sub == 3))
                    OP(DVE, lambda e: e.scalar_tensor_tensor(out=fC.t[:].rearrange("p (u q) -> p u q", u=4),
                                                             in0=bank(b).rearrange("p (u q) -> p u q", u=4),
                                                             scalar=lngT.t[:, g:g + 1],
                                                             in1=biasT.t[:, g * 128:(g + 1) * 128].unsqueeze(1).broadcast_to([128, 4, 128]),
                                                             op0=ALU.mult, op1=ALU.add), r=[pbank[b], lngT, biasT], w=[fC])
                    OP(DVE, lambda e: e.tensor_tensor(out=gB.t[:, g, :], in0=fC.t[:], in1=uT.t[:, g, :], op=ALU.mult), r=[fC, uT], w=[gB])

                g5(0)
                g5(1)
                m_qk(0)
                g5(2)
                m_pv(0)
                m_qk(1)
                g5(3)
                w_release(u)
                m_pv(1)
                m_qk(2)
                spatial(0)
                spatial(1)
                m_pv(2)
                m_qk(3)
                spatial(2)
                spatial(3)
                m_pv(3)
                NKT = NT

                def qk(sub, kt, p):
                    for kv in range(2):
                        bb = 2 * p + kv
                        pr = slice(64 * kv, 64 * kv + 64)
                        OP(PE, lambda e: e.matmul(bank(bb), lhsT=KT.t[pr, kt * 128:(kt + 1) * 128],
                                                  rhs=QT.t[pr, :, sub * 128:(sub + 1) * 128], start=True, stop=True),
                           r=[KT, QT], w=[pbank[bb]], sig=(kv == 1))

                def ex(p, q):
                    OP(ACT, lambda e: e.activation(out=PT[q].t[:], in_=psum[:, p * 1024:(p + 1) * 1024], func=AF.Exp, scale=0.125),
                       r=[pbank[2 * p], pbank[2 * p + 1]], w=[PT[q]], sig=True)

                def pv(kt, p):
                    for kv in range(2):
                        for r_ in range(2):
                            pr = slice(64 * r_, 64 * r_ + 64)
                            bb = 4 + 2 * kv + r_
                            OP(PE, lambda e: e.matmul(bank(bb), lhsT=Vaug.t[pr, kt, kv, :], rhs=PT[p].t[pr, kv * 512:(kv + 1) * 512],
                                                      start=(kt == 0), stop=(kt == NKT - 1)),
                               r=[Vaug, PT[p]], w=[pbank[bb]], sig=(kt == NKT - 1 and kv == 1 and r_ == 1))

                def epilogue(sub):
                    for kv in range(2):
                        OP(DVE, lambda e: e.tensor_copy(out=tg[kv].t[:], in_=bank(4 + 2 * kv)), r=[pbank[4 + 2 * kv]], w=[tg[kv]])
                    for kv in range(2):
                        OP(DVE, lambda e: e.tensor_tensor(out=tg[kv].t[:], in0=tg[kv].t[:], in1=bank(5 + 2 * kv), op=ALU.add),
                           r=[tg[kv], pbank[5 + 2 * kv]], w=[tg[kv]])
                    OP(DVE, lambda e: e.reciprocal(out=fC.t[0:64, :], in_=tg[0].t[64:128, :]), r=[tg[0]], w=[fC])
                    OP(DVE, lambda e: e.reciprocal(out=fC.t[64:128, :], in_=tg[1].t[0:64, :]), r=[tg[1]], w=[fC])
                    OP(DVE, lambda e: e.tensor_tensor(out=fA.t[0:64, :], in0=tg[0].t[0:64, :], in1=fC.t[0:64, :], op=ALU.mult),
                       r=[tg[0], fC], w=[fA])
                    OP(DVE, lambda e: e.tensor_tensor(out=fA.t[64:128, :], in0=tg[1].t[64:128, :], in1=fC.t[64:128, :], op=ALU.mult),
                       r=[tg[1], fC], w=[fA])
                    OP(POOL, lambda e: e.tensor_tensor(out=gA.t[:, :, sub * 128:(sub + 1) * 128],
                                                       in0=fA.t[:].rearrange("p (j t) -> p j t", j=4),
                                                       in1=zAs.t[:, :, sub * 128:(sub + 1) * 128], op=ALU.mult), r=[fA, zAs], w=[gA])

                NI = 4 * NKT

                def it(i):
                    return divmod(i, NKT)
                qk(0, 0, 0)
                ex(0, 0)
                if NI > 1:
                    qk(*it(1), 1)
                for i in range(1, NI):
                    ex(i % 2, i % 3)
                    if i + 1 < NI:
                        qk(*it(i + 1), (i + 1) % 2)
                    psub, pkt = it(i - 1)
                    pv(pkt, (i - 1) % 3)
                    if pkt == NKT - 1:
                        epilogue(psub)
                    if i % 4 == min(2, NKT - 1):
                        stage_step()
                pv(NKT - 1, (NI - 1) % 3)
                epilogue(3)

                if s_ == NST - 1:
                    stage_flush()
                br = [gA, gB, gMb]
                nxt = s_ + 1 < NST
                if nxt:
                    pend = load_x(l, (s_ + 1) * 4)
                for cb in range(8):
                    hnext = None
                    if nxt and cb < 4:
                        curx = pend
                        if cb + 1 < 4:
                            pend = load_x(l, (s_ + 1) * 4 + cb + 1)
                        tile_n = (s_ + 1) * 4 + cb
                        hnext = hT_part1(curx.t[:], curx, gN, rstd.t[:, tile_n:tile_n + 1], rstd)
                    um = muse["n"]
                    wm = wms[um % 2]
                    m_prefetch(um + 1)
                    wmv = wm.t[:].rearrange("p (n e c) -> p n e c", n=3, e=12)
                    pu = [None, None, None]
                    for n in (1, 2, 0):
                        b = next_bank()
                        for k in range(8):
                            OP(PE, lambda e: e.matmul(bank(b), lhsT=wmv[:, n, k, :], rhs=hTc.t[:, k, :], start=(k == 0), stop=(k == 7)),
                               r=[wm, hTc], w=[pbank[b]], sig=(k == 7))
                        OP(ACT, lambda e: e.activation(out=tg[n].t[:], in_=bank(b), func=AF.Tanh, scale=0.5), r=[pbank[b]], w=[tg[n]], sig=True)
                        b = next_bank()
                        for kk in range(4):
                            OP(PE, lambda e: e.matmul(bank(b), lhsT=wmv[:, n, 8 + kk, :], rhs=br[n].t[:, kk, :], start=(kk == 0), stop=(kk == 3)),
                               r=[wm, br[n]], w=[pbank[b]], sig=(kk == 3))
                        pu[n] = b
                    muse["free"][um % 2] = True
                    muse["n"] += 1
                    OP(DVE, lambda e: e.scalar_tensor_tensor(out=fA.t[:], in0=tg[1].t[:], scalar=1.0, in1=bank(pu[1]),
                                                             op0=ALU.add, op1=ALU.mult), r=[tg[1], pbank[pu[1]]], w=[fA])
                    OP(DVE, lambda e: e.scalar_tensor_tensor(out=fC.t[:], in0=tg[2].t[:], scalar=1.0, in1=bank(pu[2]),
                                                             op0=ALU.add, op1=ALU.mult), r=[tg[2], pbank[pu[2]]], w=[fC])
                    OP(DVE, lambda e: e.tensor_tensor(out=fA.t[:], in0=fA.t[:], in1=fC.t[:], op=ALU.add), r=[fA, fC], w=[fA])
                    OP(DVE, lambda e: e.scalar_tensor_tensor(out=fC.t[:], in0=tg[0].t[:], scalar=1.0, in1=bank(pu[0]),
                                                             op0=ALU.add, op1=ALU.mult), r=[tg[0], pbank[pu[0]]], w=[fC])
                    OP(DVE, lambda e: e.tensor_tensor(out=mergedT.t[:, cb, :], in0=fA.t[:], in1=fC.t[:], op=ALU.add),
                       r=[fA, fC], w=[mergedT])
                    if hnext is not None:
                        hT_part2(hnext, hT[(s_ + 1) % 2], cb * 128)

                u0 = wuse["n"]
                w_prefetch(u0 + 2)
                wo = [wsl[u0 % 3], wsl[(u0 + 1) % 3]]
                for sub in range(4):
                    tile = s_ * 4 + sub
                    s = xr_ctr[0] % 2
                    xr_ctr[0] += 1
                    xt = xr[s]
                    DMA(st_xr[s], xt.t[:], x_src(l)[tile * 128:(tile + 1) * 128, :], w=[xt])
                    for half in range(2):
                        wov = wo[half].t[:].rearrange("p (k c) -> p k c", k=8)
                        b = next_bank()
                        for k in range(8):
                            OP(PE, lambda e: e.matmul(bank(b), lhsT=mergedT.t[:, k, sub * 128:(sub + 1) * 128], rhs=wov[:, k, :],
                                                      start=(k == 0), stop=(k == 7)), r=[wo[half], mergedT], w=[pbank[b]], sig=(k == 7))
                        OP(DVE, lambda e: e.scalar_tensor_tensor(out=xt.t[:, half * 512:(half + 1) * 512], in0=bank(b), scalar=0.5,
                                                                 in1=xt.t[:, half * 512:(half + 1) * 512], op0=ALU.mult, op1=ALU.add),
                           r=[pbank[b], xt], w=[xt])
                    DMA(st_xst[s], xres_d[tile * 128:(tile + 1) * 128, :], xt.t[:], r=[xt])
                    OP(ACT, lambda e: e.activation(out=junk.t[:], in_=xt.t[:], func=AF.Square,
                                                   accum_out=ssq_nxt.t[:, tile:tile + 1]), r=[xt], w=[junk, ssq_nxt])
                w_release(u0)
                w_release(u0 + 1)
                wuse["n"] = u0 + 2
            for s in range(2):
                S.wait_stream(SP, st_xst[s])
            cur = 1 - cur


        S.new_epoch()
        ssq_cur = ssq[cur]
        DMA(st_tab, gN.t[:], fing_d.partition_broadcast(128), w=[gN])
        OP(DVE, lambda e: e.tensor_scalar(out=rstd.t[:], in0=ssq_cur.t[:], scalar1=1.0 / D, scalar2=EPS,
                                          op0=ALU.mult, op1=ALU.add), r=[ssq_cur], w=[rstd])
        OP(POOL, lambda e: e.tensor_tensor(out=rstd.t[:], in0=rstd.t[:], in1=mhalf.t[:, 0:NT], op=ALU.pow),
           r=[rstd, mhalf], w=[rstd])
        pend = load_x(1, 0)
        for i in range(NT):
            curx = pend
            if i + 1 < NT:
                pend = load_x(1, i + 1)
            s = i % 2
            OP(DVE, lambda e: e.scalar_tensor_tensor(out=xr[s].t[:], in0=curx.t[:], scalar=rstd.t[:, i:i + 1], in1=gN.t[:],
                                                     op0=ALU.mult, op1=ALU.mult), r=[curx, rstd, gN], w=[xr[s]])
            DMA(st_y, y_d[i * 128:(i + 1) * 128, :], xr[s].t[:], r=[xr[s]])
        S.wait_stream(SP, st_y)
        if debug:
            print("sbuf bytes remaining", nc.sbuf_bytes_remaining)
            print("ops", S.nops, "waits", S.nwaits, "sems", S.nsem)
    return nc


def _perm_weights(w_in, w_mem_kv, w_br, w_out):
    L = w_in.shape[0]
    out = np.empty((L, 128, WTOT), np.float32)
    for l in range(L):
        wi = w_in[l].reshape(8, 128, INW)
        o = out[l]
        o[:, OFF_KV:OFF_KV + 2048] = wi[:, :, 512:768].transpose(1, 0, 2).reshape(128, 2048)
        pair = np.concatenate([np.concatenate([np.arange(64) + 64 * j, 256 + np.arange(64) + 64 * j]) for j in range(4)])
        plain = np.arange(512)
        bases = [(0, pair), (768, pair), (1280, plain), (1792, plain), (2304, plain), (2816, plain), (3328, plain)]
        for g, (c0, cols) in enumerate(bases):
            blk = wi[:, :, c0 + cols].reshape(8, 128, 4, 128)
            o[:, OFF_G + g * 4096:OFF_G + (g + 1) * 4096] = blk.transpose(1, 2, 0, 3).reshape(128, 4096)
        wm = w_mem_kv[l].reshape(8, 128, 1024)
        o[:, OFF_MK:OFF_MK + 4096] = wm[:, :, 0:512].transpose(1, 0, 2).reshape(128, 4096)
        o[:, OFF_MV:OFF_MV + 4096] = wm[:, :, 512:1024].transpose(1, 0, 2).reshape(128, 4096)
        rowsA = np.stack([np.concatenate([64 * kk + np.arange(64), 256 + 64 * kk + np.arange(64)]) for kk in range(4)], 1)
        rowsP = np.stack([128 * kk + np.arange(128) for kk in range(4)], 1)
        for cb in range(8):
            unit = np.empty((128, 3, 12, 128), np.float32)
            for n in range(3):
                c0 = 3840 + n * 1024 + cb * 128
                unit[:, n, 0:8, :] = wi[:, :, c0:c0 + 128].transpose(1, 0, 2)
                rows = rowsA if n == 0 else rowsP
                unit[:, n, 8:12, :] = w_br[l, n][rows][:, :, cb * 128:(cb + 1) * 128]
            o[:, OFF_MG + cb * 4608:OFF_MG + (cb + 1) * 4608] = unit.reshape(128, 4608)
        wo = w_out[l].reshape(8, 128, 1024)
        for half in range(2):
            o[:, OFF_O + half * 4096:OFF_O + (half + 1) * 4096] = wo[:, :, half * 512:(half + 1) * 512].transpose(1, 0, 2).reshape(128, 4096)
    return out


def _rope_tables(seq):
    t = np.arange(seq)
    row = (t // 64).astype(np.float32)
    col = (t % 64).astype(np.float32)
    inv = (np.float32(10000.0) ** (-np.arange(16, dtype=np.float32) / np.float32(16))).astype(np.float32)
    ang = np.stack([row[:, None] * inv, col[:, None] * inv], axis=1).astype(np.float32)
    nt = seq // 128
    c = np.cos(ang).astype(np.float32).reshape(nt, 128, 32).transpose(1, 0, 2).reshape(128, nt * 32)
    s = np.sin(ang).astype(np.float32).reshape(nt, 128, 32).transpose(1, 0, 2).reshape(128, nt * 32)
    return np.ascontiguousarray(c), np.ascontiguousarray(s)


def make_in_maps(inp, L=None):
    x = np.asarray(inp["x"], np.float32)
    B, seq, _ = x.shape
    L = L or inp["w_in"].shape[0]
    wflat = _perm_weights(np.asarray(inp["w_in"])[:L], np.asarray(inp["w_mem_kv"])[:L],
                          np.asarray(inp["w_br"])[:L], np.asarray(inp["w_out"])[:L])
    c, s = _rope_tables(seq)
    wsT = np.ascontiguousarray(np.asarray(inp["w_s"])[:L].transpose(0, 3, 1, 2).reshape(L, 128, 512))
    common = {
        "wflat": wflat,
        "norm_g": np.ascontiguousarray(inp["norm_g"][:L]), "q_norm_g": np.ascontiguousarray(inp["q_norm_g"][:L]),
        "k_norm_g": np.ascontiguousarray(inp["k_norm_g"][:L]), "lngT": np.ascontiguousarray(np.asarray(inp["sg_ln_g"])[:L].reshape(L, 4, 128).transpose(0, 2, 1)),
        "lnbT": np.ascontiguousarray(np.asarray(inp["sg_ln_b"])[:L].reshape(L, 4, 128).transpose(0, 2, 1)), "wsT": wsT,
        "b_s": np.ascontiguousarray(np.asarray(inp["b_s"])[:L].reshape(L, 512)),
        "mem_norm_g": np.ascontiguousarray(inp["mem_norm_g"][:L]), "final_g": np.ascontiguousarray(inp["final_g"]),
        "ropecos": c, "ropesin": s, "ident": np.eye(128, dtype=np.float32),
    }
    maps = []
    for b in range(B):
        m = dict(common)
        m["x"] = np.ascontiguousarray(x[b])
        m["mem"] = np.ascontiguousarray(np.asarray(inp["mem"], np.float32)[b])
        maps.append(m)
    return maps


def kernel(**inputs):
    x = np.asarray(inputs["x"])
    B, seq, _ = x.shape
    L = inputs["w_in"].shape[0]
    in_maps = make_in_maps(inputs)
    nc = build(NT=seq // 128, L=L)
    res = run_bass_kernel_spmd(nc, in_maps, core_ids=list(range(B)))
    return np.stack([np.asarray(r["y"]) for r in res.results], axis=0).astype(np.float32)
```

```python
import numpy as np
from contextlib import ExitStack
import concourse.bass as bass
import concourse.mybir as mybir
from concourse.bass_utils import run_bass_kernel_spmd

F32 = mybir.dt.float32
BF16 = mybir.dt.bfloat16
AF = mybir.ActivationFunctionType
ALU = mybir.AluOpType
AX = mybir.AxisListType

D = 1024
EPS = 1e-6
INW = 6912
OFF_KV = 0
OFF_G = 2048
OFF_MK = 2048 + 7 * 4096
OFF_MV = OFF_MK + 4096
OFF_MG = OFF_MV + 4096
OFF_O = OFF_MG + 8 * 4608
WTOT = OFF_O + 2 * 4096
PIECE = 2048
NPIECE = WTOT // PIECE


class Buf:
    __slots__ = ("name", "lw", "rd")

    def __init__(self, name):
        self.name = name
        self.lw = None
        self.rd = []


class Eng:
    def __init__(self, sched, name, eng):
        self.s = sched
        self.name = name
        self.eng = eng
        self.sem = None
        self.val = 0
        self.seq = 0
        self.last = None
        self.last_sig = True
        self.sigs = []
        self.known = {}

    def new_epoch(self):
        self.sem = self.s.new_sem(self.name)
        self.val = 0

    def signal_for(self, seq):
        best = None
        for sp in reversed(self.sigs):
            if sp[0] >= seq:
                best = sp
            else:
                break
        if best is not None:
            return best
        assert self.last is not None and not self.last_sig and self.seq >= seq, (self.name, seq, self.seq)
        self.val += 1
        self.last.then_inc(self.sem, 1)
        self.last_sig = True
        self.sigs.append((self.seq, self.sem, self.val))
        if len(self.sigs) > 64:
            self.sigs = self.sigs[-32:]
        return self.sigs[-1]


class Stream:
    def __init__(self, sched, name):
        self.name = name
        self.sem = sched.new_sem(name)
        self.seq = 0

    def signal_for(self, seq):
        return (self.seq, self.sem, 16 * self.seq)


class Sched:
    def __init__(self, nc, stack):
        self.nc = nc
        self.stack = stack
        self.nsem = 0
        self.pe = Eng(self, "pe", nc.tensor)
        self.act = Eng(self, "act", nc.scalar)
        self.dve = Eng(self, "dve", nc.vector)
        self.pool = Eng(self, "pool", nc.gpsimd)
        self.sp = Eng(self, "sp", nc.sync)
        self.engs = [self.pe, self.act, self.dve, self.pool, self.sp]
        for e in self.engs:
            e.new_epoch()
        self.nwaits = 0
        self.nops = 0

    def new_sem(self, name):
        self.nsem += 1
        return self.stack.enter_context(self.nc.semaphore(f"s{self.nsem}_{name}"))

    def new_epoch(self):
        for e in self.engs:
            e.new_epoch()

    def stream(self, name):
        return Stream(self, name)

    def _deps(self, E, reads, writes, strict=False):
        deps = {}

        def add(p, seq):
            if p is E and E is self.pe and not strict:
                return
            if deps.get(p, -1) < seq:
                deps[p] = seq

        for b in reads:
            if b.lw is not None:
                add(b.lw[0], b.lw[1])
        for b in writes:
            if b.lw is not None:
                add(b.lw[0], b.lw[1])
            for (p, seq) in b.rd:
                add(p, seq)
        return deps

    def _wait(self, E, deps):
        for p, seq in deps.items():
            if E.known.get(p, -1) >= seq:
                continue
            sq, sem, val = p.signal_for(seq)
            E.eng.wait_ge(sem, val)
            self.nwaits += 1
            E.known[p] = sq

    def op(self, E, fn, reads=(), writes=(), signal=False):
        self._wait(E, self._deps(E, reads, writes))
        inst = fn(E.eng)
        self.nops += 1
        E.seq += 1
        E.last = inst
        E.last_sig = False
        if signal:
            E.signal_for(E.seq)
        me = (E, E.seq)
        for b in writes:
            b.lw = me
            b.rd = []
        for b in reads:
            b.rd.append(me)
            if len(b.rd) > 40:
                d = {}
                for (p, sq) in b.rd:
                    if d.get(p, -1) < sq:
                        d[p] = sq
                b.rd = list(d.items())
        return inst

    def dma(self, Q, stream, out, in_, reads=(), writes=()):
        self._wait(Q, self._deps(Q, reads, writes, strict=True))
        inst = Q.eng.dma_start(out=out, in_=in_)
        inst.then_inc(stream.sem, 16)
        stream.seq += 1
        self.nops += 1
        me = (stream, stream.seq)
        for b in writes:
            b.lw = me
            b.rd = []
        for b in reads:
            b.rd.append(me)
        return inst

    def wait_stream(self, E, stream):
        if stream.seq > 0:
            E.eng.wait_ge(stream.sem, 16 * stream.seq)
            E.known[stream] = stream.seq


def build(NT=32, L=4, debug=False):
    assert NT % 4 == 0
    NST = NT // 4
    STOK = NT * 128
    nc = bass.Bass("TRN2", target_bir_lowering=False)

    def din(name, shape, dt=F32):
        return nc.dram_tensor(name, list(shape), dt, kind="ExternalInput").ap()

    x_d = din("x", [STOK, D])
    mem_d = din("mem", [256, D])
    wflat_d = din("wflat", [L, 128, WTOT])
    norm_g_d = din("norm_g", [L, D])
    qg_d = din("q_norm_g", [L, 64])
    kg_d = din("k_norm_g", [L, 64])
    lngT_d = din("lngT", [L, 128, 4])
    lnbT_d = din("lnbT", [L, 128, 4])
    wsT_d = din("wsT", [L, 128, 512])
    bs_d = din("b_s", [L, 512])
    memg_d = din("mem_norm_g", [L, D])
    fing_d = din("final_g", [D])
    cos_d = din("ropecos", [128, NT * 32])
    sin_d = din("ropesin", [128, NT * 32])
    ident_d = din("ident", [128, 128])
    y_d = nc.dram_tensor("y", [STOK, D], F32, kind="ExternalOutput").ap()
    xres_d = nc.dram_tensor("xres", [STOK, D], F32, kind="Internal").ap()
    wbf_d = nc.dram_tensor("wbf", [L, 128, WTOT], BF16, kind="Internal").ap()

    with ExitStack() as st:
        S = Sched(nc, st)
        PE, ACT, DVE, POOL, SP = S.pe, S.act, S.dve, S.pool, S.sp

        def sb(name, shape, dt):
            return st.enter_context(nc.sbuf_tensor("sb_" + name, list(shape), dt))

        class T:
            def __init__(self, name, shape, dt):
                self.t = sb(name, shape, dt)
                self.b = Buf(name)

        psum = st.enter_context(nc.psum_tensor("psum", [128, 4096], F32))
        pbank = [Buf(f"bank{i}") for i in range(8)]
        bank_rr = [0]

        def bank(i):
            return psum[:, i * 512:(i + 1) * 512]

        def bankb(i):
            return psum[:, i * 512:(i + 1) * 512].bitcast(BF16)

        def next_bank(allowed=(0, 1, 2, 3, 4, 5, 6, 7)):
            while True:
                bank_rr[0] = (bank_rr[0] + 1) % 8
                if bank_rr[0] in allowed:
                    return bank_rr[0]

        class Alias(T):
            def __init__(self, ap, buf):
                self.t = ap
                self.b = buf

        identf = T("identf", [128, 128], F32)
        identb = T("identb", [128, 128], BF16)
        onesb = T("onesb", [128, 128], BF16)
        mhalf = T("mhalf", [128, 32], F32)
        cosT = T("cosT", [128, NT * 32], F32)
        sinT = T("sinT", [128, NT * 32], F32)
        gN = T("gN", [128, D], F32)
        qg = T("qg", [128, 64], F32)
        kg = T("kg", [128, 64], F32)
        lngT = T("lngT", [128, 4], F32)
        lnbT = T("lnbT", [128, 4], F32)
        biasT = T("biasT", [128, 512], F32)
        wsT = T("wsT", [128, 512], BF16)
        memss = T("memss", [128, 2], F32)
        memrs = T("memrs", [128, 2], F32)
        KmT = T("KmT", [128, 4, 256], BF16)
        Vm = T("Vm", [128, 2, 512], BF16)
        KT = T("KT", [128, STOK], BF16)
        Vaug = T("Vaug", [128, NT, 2, 128], BF16)
        ssq = [T("ssqA", [128, NT], F32), T("ssqB", [128, NT], F32)]
        rstd = T("rstd", [128, NT], F32)
        junk = T("junk", [128, D], BF16)
        xin = [T(f"xin{i}", [128, D], F32) for i in range(2)]
        hb = [T(f"hb{i}", [128, D], BF16) for i in range(2)]
        hT = [T(f"hT{i}", [128, 8, 512], BF16) for i in range(2)]
        wsl = [T(f"wsl{i}", [128, 4096], BF16) for i in range(3)]
        wms = [T(f"wms{i}", [128, 4608], BF16) for i in range(2)]
        fA = T("fA", [128, 512], F32)
        fB = T("fB", [128, 512], F32)
        fC = T("fC", [128, 512], F32)
        fD = T("fD", [128, 512], F32)
        sm8 = T("sm8", [128, 8], F32)
        sm8b = T("sm8b", [128, 8], F32)
        ssq_q = T("ssq_q", [128, 32], F32)
        rq = T("rq", [128, 32], F32)
        st6 = T("st6", [128, 4, 6], F32)
        mv4 = T("mv4", [128, 4, 2], F32)
        rs4 = T("rs4", [128, 4], F32)
        nb4 = T("nb4", [128, 4], F32)
        QT = T("QT", [128, 4, 512], BF16)
        zAs = T("zAs", [128, 4, 512], BF16)
        uT = T("uT", [128, 4, 512], BF16)
        zBs = T("zBs", [128, 4, 512], BF16)
        qMT = T("qMT", [128, 4, 512], BF16)
        zMs = T("zMs", [128, 4, 512], BF16)
        vnb = T("vnb", [128, 4, 512], BF16)
        gA = T("gA", [128, 4, 512], BF16)
        gB = T("gB", [128, 4, 512], BF16)
        gMb = T("gMb", [128, 4, 512], BF16)
        PTt = sb("PTt", [128, 3072], BF16)
        PT = [Alias(PTt[:, i * 1024:(i + 1) * 1024], Buf(f"PT{i}")) for i in range(3)]
        PmT = [T(f"PmT{i}", [128, 512], BF16) for i in range(2)]
        tg = [T(f"tg{i}", [128, 512], F32) for i in range(3)]
        mergedT = T("mergedT", [128, 8, 512], BF16)
        xr = [T(f"xr{i}", [128, D], F32) for i in range(2)]
        memT = Alias(zAs.t[:].rearrange("p a (b c) -> p (a b) c", b=2), zAs.b)
        qf4 = Alias(mergedT.t[:].rearrange("p k t -> p (k t)").bitcast(F32), mergedT.b)
        qrotk = [Alias(PTt[:, 0:512], PT[0].b), Alias(PTt[:, 1024:1536], PT[1].b)]
        PIECE2 = 512
        NP2 = WTOT // PIECE2
        sf = [T(f"sf{i}", [128, PIECE2], F32) for i in range(2)]
        sbf = [T(f"sbf{i}", [128, PIECE2], BF16) for i in range(2)]
        wbfB = [[Buf(f"wbf{l}_{i}") for i in range(2)] for l in range(L)]
        stg_f_ap = [wms[i].t[:].bitcast(F32)[:, 0:PIECE] for i in range(2)]
        stg_b_ap = [wsl[i].t[:, 0:PIECE] for i in range(2)]

        st_setup = S.stream("setup")
        st_xin = [S.stream(f"xin{i}") for i in range(2)]
        st_xr = [S.stream(f"xr{i}") for i in range(2)]
        st_wsl = [S.stream(f"wsl{i}") for i in range(3)]
        st_wms = [S.stream(f"wms{i}") for i in range(2)]
        st_stgl = [S.stream(f"stgl{i}") for i in range(2)]
        st_stgs = [S.stream(f"stgs{i}") for i in range(2)]
        st_sfl = [S.stream(f"sfl{i}") for i in range(2)]
        st_sfs = [S.stream(f"sfs{i}") for i in range(2)]
        st_xst = [S.stream(f"xst{i}") for i in range(2)]
        st_tab = S.stream("tab")
        st_y = S.stream("y")

        def OP(E, fn, r=(), w=(), sig=False):
            return S.op(E, fn, reads=[t.b if isinstance(t, T) else t for t in r],
                        writes=[t.b if isinstance(t, T) else t for t in w], signal=sig)

        def DMA(stream, out, in_, r=(), w=()):
            return S.dma(SP, stream, out, in_, reads=[t.b if isinstance(t, T) else t for t in r],
                         writes=[t.b if isinstance(t, T) else t for t in w])

        DMA(st_setup, identf.t[:], ident_d, w=[identf])
        DMA(st_setup, cosT.t[:], cos_d, w=[cosT])
        DMA(st_setup, sinT.t[:], sin_d, w=[sinT])
        OP(DVE, lambda e: e.tensor_copy(out=identb.t[:], in_=identf.t[:]), r=[identf], w=[identb])
        OP(POOL, lambda e: e.memset(mhalf.t[:], -0.5), w=[mhalf])
        OP(POOL, lambda e: e.memset(onesb.t[:], 1.0), w=[onesb])
        OP(POOL, lambda e: e.memset(Vaug.t[:, :, 0, 64:128], 1.0), w=[Vaug])
        OP(POOL, lambda e: e.memset(Vaug.t[:, :, 1, 0:64], 1.0), w=[Vaug])
        xin_ctr = [0]

        def load_tile(src_ap):
            s = xin_ctr[0] % 2
            xin_ctr[0] += 1
            DMA(st_xin[s], xin[s].t[:], src_ap, w=[xin[s]])
            return xin[s]

        for t in range(2):
            mt = load_tile(mem_d[t * 128:(t + 1) * 128, :])
            OP(ACT, lambda e: e.activation(out=junk.t[:], in_=mt.t[:], func=AF.Square,
                                           accum_out=memss.t[:, t:t + 1]), r=[mt], w=[junk, memss])
        OP(DVE, lambda e: e.tensor_scalar(out=memrs.t[:], in0=memss.t[:], scalar1=1.0 / D, scalar2=EPS,
                                          op0=ALU.mult, op1=ALU.add), r=[memss], w=[memrs])
        OP(POOL, lambda e: e.tensor_tensor(out=memrs.t[:], in0=memrs.t[:], in1=mhalf.t[:, 0:2], op=ALU.pow),
           r=[memrs, mhalf], w=[memrs])

        cast_engs = [POOL, DVE, ACT]
        npc = 0
        up_pieces = [OFF_KV // PIECE] + list(range(OFF_MK // PIECE, OFF_MG // PIECE))
        for l in range(1):
            for pi in up_pieces:
                s = npc % 2
                ce = cast_engs[npc % 3]
                DMA(st_stgl[s], stg_f_ap[s], wflat_d[l, :, pi * PIECE:(pi + 1) * PIECE], w=[wms[s]])
                if ce is ACT:
                    OP(ACT, lambda e: e.activation(out=stg_b_ap[s], in_=stg_f_ap[s], func=AF.Copy),
                       r=[wms[s]], w=[wsl[s]])
                else:
                    OP(ce, lambda e: e.tensor_copy(out=stg_b_ap[s], in_=stg_f_ap[s]), r=[wms[s]], w=[wsl[s]])
                DMA(st_stgs[s], wbf_d[l, :, pi * PIECE:(pi + 1) * PIECE], stg_b_ap[s], r=[wsl[s]])
                npc += 1
        for s in range(2):
            S.wait_stream(SP, st_stgs[s])

        def x_src(l):
            return x_d if l == 0 else xres_d

        def load_x(l, tile):
            return load_tile(x_src(l)[tile * 128:(tile + 1) * 128, :])


        wuse = {"n": 0, "loaded": 0, "plan": [], "free": [True, True, True]}
        muse = {"n": 0, "loaded": 0, "plan": [], "free": [True, True]}
        for l in range(L):
            wuse["plan"] += [(l, OFF_MK, 4096), (l, OFF_MV, 4096), (l, OFF_KV, 2048)]
            for s_ in range(NST):
                wuse["plan"] += [(l, OFF_G + g * 4096, 4096) for g in range(7)]
                wuse["plan"] += [(l, OFF_O, 4096), (l, OFF_O + 4096, 4096)]
                muse["plan"] += [(l, OFF_MG + cb * 4608, 4608) for cb in range(8)]

        def w_prefetch(upto):
            while wuse["loaded"] <= min(upto, len(wuse["plan"]) - 1):
                u = wuse["loaded"]
                s = u % 3
                assert wuse["free"][s], ("w slot not released", u)
                l, off, n = wuse["plan"][u]
                stage_need(l, off, n)
                DMA(st_wsl[s], wsl[s].t[:, 0:n], wbf_d[l, :, off:off + n], r=wbfB[l], w=[wsl[s]])
                wuse["free"][s] = False
                wuse["loaded"] += 1

        def w_get():
            u = wuse["n"]
            assert u < wuse["loaded"]
            return u, wsl[u % 3]

        def w_release(u):
            wuse["free"][u % 3] = True
            if u == wuse["n"]:
                wuse["n"] += 1

        def m_prefetch(upto):
            while muse["loaded"] <= min(upto, len(muse["plan"]) - 1):
                u = muse["loaded"]
                s = u % 2
                assert muse["free"][s]
                l, off, n = muse["plan"][u]
                stage_need(l, off, n)
                DMA(st_wms[s], wms[s].t[:, 0:n], wbf_d[l, :, off:off + n], r=wbfB[l], w=[wms[s]])
                muse["free"][s] = False
                muse["loaded"] += 1

        stg = {"l": None, "i": 0, "pieces": []}
        stg_done = [set() for _ in range(L)]

        def stage_need(l, off, n):
            need = range(off // PIECE2, (off + n + PIECE2 - 1) // PIECE2)
            while any(k not in stg_done[l] for k in need):
                assert stg["l"] == l, ("staging for layer not active", l, stg["l"])
                stage_step()

        def stage_begin(l, pieces=None):
            stg["l"] = l
            stg["i"] = 0
            stg["pieces"] = list(range(NP2)) if pieces is None else list(pieces)

        def stage_step(use_dve=False):
            l = stg["l"]
            if l is None:
                return
            i = stg["i"]
            pcs = stg["pieces"]
            n = len(pcs)
            if i >= n + 2:
                stg["l"] = None
                return
            if i < n:
                DMA(st_sfl[i % 2], sf[i % 2].t[:], wflat_d[l, :, pcs[i] * PIECE2:(pcs[i] + 1) * PIECE2], w=[sf[i % 2]])
            j = i - 1
            if 0 <= j < n:
                ce = DVE if (use_dve and j % 2 == 1) else POOL
                OP(ce, lambda e: e.tensor_copy(out=sbf[j % 2].t[:], in_=sf[j % 2].t[:]), r=[sf[j % 2]], w=[sbf[j % 2]])
            k = i - 2
            if 0 <= k < n:
                S.dma(SP, st_sfs[k % 2], wbf_d[l, :, pcs[k] * PIECE2:(pcs[k] + 1) * PIECE2], sbf[k % 2].t[:],
                      reads=[sbf[k % 2].b], writes=[wbfB[l][k % 2]])
                stg_done[l].add(pcs[k])
            stg["i"] += 1

        def stage_flush():
            while stg["l"] is not None:
                stage_step()

        up_set = set()
        for pi in up_pieces:
            up_set.update(range(pi * (PIECE // PIECE2), (pi + 1) * (PIECE // PIECE2)))
        stg_done[0].update(up_set)
        stage_begin(0, [k for k in range(NP2) if k not in up_set])
        pend = load_x(0, 0)
        for i in range(NT):
            cur = pend
            if i + 1 < NT:
                pend = load_x(0, i + 1)
            OP(ACT, lambda e: e.activation(out=junk.t[:], in_=cur.t[:], func=AF.Square,
                                           accum_out=ssq[0].t[:, i:i + 1]), r=[cur], w=[junk, ssq[0]])
            stage_step(True)
            stage_step(True)
            stage_step(True)

        hb_ctr = [0]

        def hT_part1(xt_ap, xt_T, gtab, rs_ap, rs_T):
            s = hb_ctr[0] % 2
            hb_ctr[0] += 1
            h = hb[s]
            OP(DVE, lambda e: e.scalar_tensor_tensor(out=h.t[:], in0=xt_ap, scalar=rs_ap, in1=gtab.t[:],
                                                     op0=ALU.mult, op1=ALU.mult), r=[gtab, rs_T, xt_T], w=[h])
            return h

        def hT_part2(h, dstT, col0, banks=(0, 1, 2, 3, 4, 5, 6, 7)):
            b = next_bank(banks)
            for k in range(8):
                OP(PE, lambda e: e.transpose(out=bankb(b)[:, k * 128:(k + 1) * 128], in_=h.t[:, k * 128:(k + 1) * 128],
                                             identity=identb.t[:]), r=[h, identb], w=[pbank[b]], sig=(k == 7))
            OP(DVE, lambda e: e.tensor_copy(out=dstT.t[:, :, col0:col0 + 128],
                                            in_=bankb(b).rearrange("p (k t) -> p k t", k=8)), r=[pbank[b]], w=[dstT])

        def make_hT(xt_ap, xt_T, gtab, rs_ap, rs_T, dstT, col0, banks=(0, 1, 2, 3, 4, 5, 6, 7)):
            hT_part2(hT_part1(xt_ap, xt_T, gtab, rs_ap, rs_T), dstT, col0, banks)

        def q3(ap):
            return ap.rearrange("p (h d) -> p h d", h=8)

        def norm8(src, gtab):
            OP(DVE, lambda e: e.tensor_tensor(out=fA.t[:], in0=src.t[:], in1=src.t[:], op=ALU.mult), r=[src], w=[fA])
            OP(DVE, lambda e: e.tensor_reduce(out=sm8.t[:], in_=q3(fA.t[:]), axis=AX.X, op=ALU.add), r=[fA], w=[sm8])
            OP(DVE, lambda e: e.tensor_scalar(out=sm8b.t[:], in0=sm8.t[:], scalar1=1.0 / 64, scalar2=EPS,
                                              op0=ALU.mult, op1=ALU.add), r=[sm8], w=[sm8b])
            OP(POOL, lambda e: e.tensor_tensor(out=sm8b.t[:], in0=sm8b.t[:], in1=mhalf.t[:, 0:8], op=ALU.pow),
               r=[sm8b, mhalf], w=[sm8b])
            OP(DVE, lambda e: e.tensor_tensor(out=q3(src.t[:]), in0=q3(src.t[:]),
                                              in1=sm8b.t[:].unsqueeze(2).broadcast_to([128, 8, 64]), op=ALU.mult),
               r=[src, sm8b], w=[src])
            OP(DVE, lambda e: e.tensor_tensor(out=q3(src.t[:]), in0=q3(src.t[:]),
                                              in1=gtab.t[:].unsqueeze(1).broadcast_to([128, 8, 64]), op=ALU.mult),
               r=[src, gtab], w=[src])

        def rope_q4(tile0, U=4, H=8):
            n = U * H * 64
            m = U * H * 16

            def xv(ap, sg, r):
                return ap.rearrange("p (u h s r f) -> p u h s r f", u=U, h=H, s=2, r=2)[:, :, :, sg, r, :]

            def tv(t):
                return t.t[:, 0:m].rearrange("p (u h f) -> p u h f", u=U, h=H)

            def tb(t, sg):
                return t.t[:, tile0 * 32:(tile0 + U) * 32].rearrange("p (u s f) -> p u s f", u=U, s=2)[:, :, sg, :] \
                    .unsqueeze(2).broadcast_to([128, U, H, 16])
            src = qf4.t[:, 0:n]
            dst = PTt[:, 0:n]
            PTs = [PT[0], PT[1]]
            for sg in range(2):
                OP(DVE, lambda e: e.tensor_tensor(out=tv(fA), in0=xv(src, sg, 0), in1=tb(cosT, sg), op=ALU.mult), r=[qf4, cosT], w=[fA])
                OP(DVE, lambda e: e.tensor_tensor(out=tv(fC), in0=xv(src, sg, 1), in1=tb(sinT, sg), op=ALU.mult), r=[qf4, sinT], w=[fC])
                OP(DVE, lambda e: e.tensor_tensor(out=xv(dst, sg, 0), in0=tv(fA), in1=tv(fC), op=ALU.subtract), r=[fA, fC], w=PTs)
                OP(DVE, lambda e: e.tensor_tensor(out=tv(fA), in0=xv(src, sg, 1), in1=tb(cosT, sg), op=ALU.mult), r=[qf4, cosT], w=[fA])
                OP(DVE, lambda e: e.tensor_tensor(out=tv(fC), in0=xv(src, sg, 0), in1=tb(sinT, sg), op=ALU.mult), r=[qf4, sinT], w=[fC])
                OP(DVE, lambda e: e.tensor_tensor(out=xv(dst, sg, 1), in0=tv(fA), in1=tv(fC), op=ALU.add), r=[fA, fC], w=PTs)

        def rope_k(src, dst, tile0):
            def v5(t, hd, r):
                return t.t[:].rearrange("p (u h s r f) -> p u h s r f", u=4, h=2, s=2, r=2)[:, :, hd, :, r, :]

            def v4(t, hd):
                return t.t[:, 0:256].rearrange("p (u h s f) -> p u h s f", u=4, h=2, s=2)[:, :, hd, :, :]

            def tb(t):
                return t.t[:, tile0 * 32:(tile0 + 4) * 32].rearrange("p (u s f) -> p u s f", u=4, s=2)
            for hd in range(2):
                OP(DVE, lambda e: e.tensor_tensor(out=v4(fA, hd), in0=v5(src, hd, 0), in1=tb(cosT), op=ALU.mult), r=[src, cosT], w=[fA])
                OP(DVE, lambda e: e.tensor_tensor(out=v4(fC, hd), in0=v5(src, hd, 1), in1=tb(sinT), op=ALU.mult), r=[src, sinT], w=[fC])
                OP(DVE, lambda e: e.tensor_tensor(out=v5(dst, hd, 0), in0=v4(fA, hd), in1=v4(fC, hd), op=ALU.subtract), r=[fA, fC], w=[dst])
                OP(DVE, lambda e: e.tensor_tensor(out=v4(fA, hd), in0=v5(src, hd, 1), in1=tb(cosT), op=ALU.mult), r=[src, cosT], w=[fA])
                OP(DVE, lambda e: e.tensor_tensor(out=v4(fC, hd), in0=v5(src, hd, 0), in1=tb(sinT), op=ALU.mult), r=[src, sinT], w=[fC])
                OP(DVE, lambda e: e.tensor_tensor(out=v5(dst, hd, 1), in0=v4(fA, hd), in1=v4(fC, hd), op=ALU.add), r=[fA, fC], w=[dst])

        def silu2_evac(b, dst_ap, dstT):
            OP(ACT, lambda e: e.activation(out=dst_ap, in_=bank(b), func=AF.Silu), r=[pbank[b]], w=[dstT])

        def emit_hT_sub(l, tile, dstT, sub, curx):
            make_hT(curx.t[:], curx, gN, rstd.t[:, tile:tile + 1], rstd, dstT, sub * 128)

        cur = 0
        for l in range(L):
            if l > 0:
                S.new_epoch()
            ssq_cur, ssq_nxt = ssq[cur], ssq[1 - cur]
            gM = xr[0]
            DMA(st_tab, gN.t[:], norm_g_d[l].partition_broadcast(128), w=[gN])
            DMA(st_tab, gM.t[:], memg_d[l].partition_broadcast(128), w=[gM])
            DMA(st_tab, qg.t[:], qg_d[l].partition_broadcast(128), w=[qg])
            DMA(st_tab, kg.t[:], kg_d[l].partition_broadcast(128), w=[kg])
            DMA(st_tab, lngT.t[:], lngT_d[l], w=[lngT])
            DMA(st_tab, lnbT.t[:], lnbT_d[l], w=[lnbT])
            DMA(st_tab, biasT.t[:], bs_d[l].partition_broadcast(128), w=[biasT])
            DMA(st_tab, fD.t[:], wsT_d[l], w=[fD])
            OP(DVE, lambda e: e.tensor_copy(out=wsT.t[:], in_=fD.t[:]), r=[fD], w=[wsT])
            OP(DVE, lambda e: e.tensor_scalar(out=rstd.t[:], in0=ssq_cur.t[:], scalar1=1.0 / D, scalar2=EPS,
                                              op0=ALU.mult, op1=ALU.add), r=[ssq_cur], w=[rstd])
            OP(POOL, lambda e: e.tensor_tensor(out=rstd.t[:], in0=rstd.t[:], in1=mhalf.t[:, 0:NT], op=ALU.pow),
               r=[rstd, mhalf], w=[rstd])
            b = next_bank()
            OP(PE, lambda e: e.matmul(bank(b), lhsT=onesb.t[:], rhs=wsT.t[:], start=True, stop=True), r=[onesb, wsT], w=[pbank[b]], sig=True)
            for g in range(4):
                OP(DVE, lambda e: e.scalar_tensor_tensor(out=biasT.t[:, g * 128:(g + 1) * 128], in0=bank(b)[:, g * 128:(g + 1) * 128],
                                                         scalar=lnbT.t[:, g:g + 1], in1=biasT.t[:, g * 128:(g + 1) * 128],
                                                         op0=ALU.mult, op1=ALU.add), r=[pbank[b], lnbT, biasT], w=[biasT])

            w_prefetch(wuse["n"] + 2)
            for t in range(2):
                mt = load_tile(mem_d[t * 128:(t + 1) * 128, :])
                make_hT(mt.t[:], mt, gM, memrs.t[:, t:t + 1], memrs, memT, t * 128)
            u, ws = w_get()
            wv = ws.t[:].rearrange("p (k c) -> p k c", k=8)
            for h in range(4):
                b = next_bank()
                for k in range(8):
                    OP(PE, lambda e: e.matmul(bank(b)[:, 0:256], lhsT=wv[:, k, h * 128:(h + 1) * 128], rhs=memT.t[:, k, :],
                                              start=(k == 0), stop=(k == 7)), r=[ws, memT], w=[pbank[b]], sig=(k == 7))
                OP(DVE, lambda e: e.tensor_copy(out=KmT.t[:, h, :], in_=bank(b)[:, 0:256]), r=[pbank[b]], w=[KmT])
            w_release(u)
            w_prefetch(wuse["n"] + 2)
            u, ws = w_get()
            wv = ws.t[:].rearrange("p (k c) -> p k c", k=8)
            for t in range(2):
                b = next_bank()
                for k in range(8):
                    OP(PE, lambda e: e.matmul(bank(b), lhsT=memT.t[:, k, t * 128:(t + 1) * 128], rhs=wv[:, k, :],
                                              start=(k == 0), stop=(k == 7)), r=[ws, memT], w=[pbank[b]], sig=(k == 7))
                OP(DVE, lambda e: e.tensor_copy(out=Vm.t[:, t, :], in_=bank(b)), r=[pbank[b]], w=[Vm])
            w_release(u)
            w_prefetch(wuse["n"] + 2)

            u, ws = w_get()
            wkv = ws.t[:, 0:2048].rearrange("p (k c) -> p k c", k=8)
            pend = load_x(l, 0)
            hq = [None]
            NBT = min(16, NT)
            for bt in range(NT // NBT):
                for sb in range(NBT // 4):
                    s_ = bt * (NBT // 4) + sb
                    hTc = hT[s_ % 2]
                    for sub in range(4):
                        tile = s_ * 4 + sub
                        if hq[0] is None:
                            curx = pend
                            if tile + 1 < NT:
                                pend = load_x(l, tile + 1)
                            hq[0] = hT_part1(curx.t[:], curx, gN, rstd.t[:, tile:tile + 1], rstd)
                        hcur = hq[0]
                        hq[0] = None
                        if tile + 1 < NT:
                            curx = pend
                            if tile + 2 < NT:
                                pend = load_x(l, tile + 2)
                            hq[0] = hT_part1(curx.t[:], curx, gN, rstd.t[:, tile + 1:tile + 2], rstd)
                        hT_part2(hcur, hTc, sub * 128, banks=(2, 3, 4, 5, 6, 7))
                        stage_step()
                        stage_step()
                        stage_step()
                    for sub in range(4):
                        bb = sub // 2
                        o = bank(bb)[:, (sub % 2) * 256:(sub % 2) * 256 + 256]
                        for k in range(8):
                            OP(PE, lambda e: e.matmul(o, lhsT=hTc.t[:, k, sub * 128:(sub + 1) * 128], rhs=wkv[:, k, :],
                                                      start=(k == 0), stop=(k == 7)), r=[ws, hTc], w=[pbank[bb]], sig=(k == 7))
                    pkv = psum[:, 0:1024].rearrange("p (u c) -> p u c", u=4)
                    OP(ACT, lambda e: e.activation(out=Vaug.t[:, s_ * 4:s_ * 4 + 4, 0, 0:64], in_=pkv[:, :, 128:192], func=AF.Copy),
                       r=[pbank[0], pbank[1]], w=[Vaug])
                    OP(ACT, lambda e: e.activation(out=Vaug.t[:, s_ * 4:s_ * 4 + 4, 1, 64:128], in_=pkv[:, :, 192:256], func=AF.Copy),
                       r=[pbank[0], pbank[1]], w=[Vaug])
                    OP(ACT, lambda e: e.activation(out=qf4.t[:, sb * 512:(sb + 1) * 512].rearrange("p (u c) -> p u c", u=4),
                                                   in_=pkv[:, :, 0:128], func=AF.Copy), r=[pbank[0], pbank[1]], w=[qf4])
                    for sub in range(4):
                        for hd in range(2):
                            ci = (sb * 4 + sub) * 2 + hd
                            OP(ACT, lambda e: e.activation(out=junk.t[:, 0:64], in_=pkv[:, sub, hd * 64:(hd + 1) * 64], func=AF.Square,
                                                           accum_out=ssq_q.t[:, ci:ci + 1]), r=[pbank[0], pbank[1]], w=[junk, ssq_q])
                nh = NBT * 2
                OP(DVE, lambda e: e.tensor_scalar(out=rq.t[:, 0:nh], in0=ssq_q.t[:, 0:nh], scalar1=1.0 / 64, scalar2=EPS,
                                                  op0=ALU.mult, op1=ALU.add), r=[ssq_q], w=[rq])
                OP(POOL, lambda e: e.tensor_tensor(out=rq.t[:, 0:nh], in0=rq.t[:, 0:nh], in1=mhalf.t[:, 0:nh], op=ALU.pow), r=[rq, mhalf], w=[rq])
                kfv = qf4.t[:, 0:nh * 64].rearrange("p (h d) -> p h d", h=nh)
                OP(DVE, lambda e: e.tensor_tensor(out=kfv, in0=kfv, in1=rq.t[:, 0:nh].unsqueeze(2).broadcast_to([128, nh, 64]), op=ALU.mult),
                   r=[qf4, rq], w=[qf4])
                OP(DVE, lambda e: e.tensor_tensor(out=kfv, in0=kfv, in1=kg.t[:].unsqueeze(1).broadcast_to([128, nh, 64]), op=ALU.mult),
                   r=[qf4, kg], w=[qf4])
                rope_q4(bt * NBT, U=NBT, H=2)
                for hb_ in range((NBT + 7) // 8):
                    nt_ = min(8, NBT - hb_ * 8)
                    b = next_bank((2, 3, 4, 5, 6, 7))
                    for t_ in range(nt_):
                        idx = hb_ * 8 + t_
                        OP(PE, lambda e: e.transpose(out=bankb(b)[:, t_ * 128:(t_ + 1) * 128], in_=PTt[:, idx * 128:(idx + 1) * 128],
                                                     identity=identb.t[:]), r=[PT[0], PT[1], identb], w=[pbank[b]], sig=(t_ == nt_ - 1))
                    c0 = (bt * NBT + hb_ * 8) * 128
                    OP(DVE, lambda e: e.tensor_copy(out=KT.t[:, c0:c0 + nt_ * 128], in_=bankb(b)[:, 0:nt_ * 128]), r=[pbank[b]], w=[KT])
            w_release(u)

            xr_ctr = [0]
            stage_flush()
            if l + 1 < L:
                stage_begin(l + 1)
            pend = load_x(l, 0)
            for sub in range(4):
                curx = pend
                if sub + 1 < 4:
                    pend = load_x(l, sub + 1)
                emit_hT_sub(l, sub, hT[0], sub, curx)
            for s_ in range(NST):
                hTc = hT[s_ % 2]
                w_prefetch(wuse["n"] + 1)
                m_prefetch(muse["n"] + 1)

                def fm_block(ws, wv4, blk):
                    b = next_bank()
                    for k in range(8):
                        OP(PE, lambda e: e.matmul(bank(b), lhsT=wv4[:, blk, k, :], rhs=hTc.t[:, k, :],
                                                  start=(k == 0), stop=(k == 7)), r=[ws, hTc], w=[pbank[b]], sig=(k == 7))
                    return b

                def tm_sub(ws, wv4, sub):
                    b = next_bank()
                    for k in range(8):
                        OP(PE, lambda e: e.matmul(bank(b), lhsT=hTc.t[:, k, sub * 128:(sub + 1) * 128], rhs=wv4[:, :, k, :],
                                                  start=(k == 0), stop=(k == 7)), r=[ws, hTc], w=[pbank[b]], sig=(k == 7))
                    return b

                def wview(ws):
                    return ws.t[:].rearrange("p (b k c) -> p b k c", b=4, k=8)

                u, ws = w_get()
                w_prefetch(u + 2)
                qb = []
                for sub in range(4):
                    b = tm_sub(ws, wview(ws), sub)
                    qb.append(b)
                    for hh in range(8):
                        OP(ACT, lambda e: e.activation(out=junk.t[:, 0:64], in_=bank(b)[:, hh * 64:(hh + 1) * 64], func=AF.Square,
                                                       accum_out=ssq_q.t[:, sub * 8 + hh:sub * 8 + hh + 1]),
                           r=[pbank[b]], w=[junk, ssq_q])
                w_release(u)
                OP(DVE, lambda e: e.tensor_scalar(out=rq.t[:], in0=ssq_q.t[:], scalar1=1.0 / 64, scalar2=EPS,
                                                  op0=ALU.mult, op1=ALU.add), r=[ssq_q], w=[rq])
                OP(POOL, lambda e: e.tensor_tensor(out=rq.t[:], in0=rq.t[:], in1=mhalf.t[:, 0:32], op=ALU.pow), r=[rq, mhalf], w=[rq])
                u, ws = w_get()
                w_prefetch(u + 2)
                for blk in range(4):
                    b = fm_block(ws, wview(ws), blk)
                    silu2_evac(b, zAs.t[:, blk, :], zAs)
                w_release(u)
                for sub in range(4):
                    OP(DVE, lambda e: e.tensor_tensor(out=q3(qf4.t[:, sub * 512:(sub + 1) * 512]), in0=q3(bank(qb[sub])),
                                                      in1=rq.t[:, sub * 8:(sub + 1) * 8].unsqueeze(2).broadcast_to([128, 8, 64]),
                                                      op=ALU.mult), r=[pbank[qb[sub]], rq], w=[qf4])
                OP(DVE, lambda e: e.tensor_tensor(out=qf4.t[:].rearrange("p (h d) -> p h d", h=32),
                                                  in0=qf4.t[:].rearrange("p (h d) -> p h d", h=32),
                                                  in1=qg.t[:].unsqueeze(1).broadcast_to([128, 32, 64]), op=ALU.mult), r=[qf4, qg], w=[qf4])
                rope_q4(s_ * 4)
                u, ws = w_get()
                w_prefetch(u + 2)
                for blk in range(4):
                    b = fm_block(ws, wview(ws), blk)
                    OP(ACT, lambda e: e.activation(out=uT.t[:, blk, :], in_=bank(b), func=AF.Copy), r=[pbank[b]], w=[uT])
                w_release(u)
                u, ws = w_get()
                w_prefetch(u + 2)
                vb = []
                for sub in range(4):
                    b = tm_sub(ws, wview(ws), sub)
                    vb.append(b)
                    OP(DVE, lambda e: e.bn_stats(out=st6.t[:, sub, :], in_=bank(b)), r=[pbank[b]], w=[st6])
                    OP(DVE, lambda e: e.bn_aggr(out=mv4.t[:, sub, :], in_=st6.t[:, sub, :]), r=[st6], w=[mv4])
                w_release(u)
                OP(DVE, lambda e: e.tensor_scalar(out=rs4.t[:], in0=mv4.t[:, :, 1], scalar1=EPS, scalar2=None, op0=ALU.add),
                   r=[mv4], w=[rs4])
                OP(POOL, lambda e: e.tensor_tensor(out=rs4.t[:], in0=rs4.t[:], in1=mhalf.t[:, 0:4], op=ALU.pow), r=[rs4, mhalf], w=[rs4])
                OP(DVE, lambda e: e.scalar_tensor_tensor(out=nb4.t[:], in0=mv4.t[:, :, 0], scalar=-1.0, in1=rs4.t[:],
                                                         op0=ALU.mult, op1=ALU.mult), r=[mv4, rs4], w=[nb4])
                for sub in range(4):
                    OP(ACT, lambda e: e.activation(out=vnb.t[:, sub, :], in_=bank(vb[sub]), func=AF.Identity,
                                                   scale=rs4.t[:, sub:sub + 1], bias=nb4.t[:, sub:sub + 1]),
                       r=[pbank[vb[sub]], nb4, rs4], w=[vnb])
                u, ws = w_get()
                w_prefetch(u + 2)
                for blk in range(4):
                    b = fm_block(ws, wview(ws), blk)
                    silu2_evac(b, zBs.t[:, blk, :], zBs)
                w_release(u)
                for g in range(4):
                    OP(POOL, lambda e: e.tensor_tensor(out=uT.t[:, g, :], in0=uT.t[:, g, :], in1=zBs.t[:, g, :], op=ALU.mult),
                       r=[uT, zBs], w=[uT])
                for g in range(4):
                    b = next_bank()
                    for sub in range(4):
                        OP(PE, lambda e: e.matmul(bank(b)[:, sub * 128:(sub + 1) * 128], lhsT=vnb.t[:, sub, g * 128:(g + 1) * 128],
                                                  rhs=wsT.t[:, g * 128:(g + 1) * 128], start=True, stop=True),
                           r=[vnb, wsT], w=[pbank[b]], sig=(sub == 3))
                    OP(DVE, lambda e: e.scalar_tensor_tensor(out=fC.t[:].rearrange("p (u q) -> p u q", u=4),
                                                             in0=bank(b).rearrange("p (u q) -> p u q", u=4),
                                                             scalar=lngT.t[:, g:g + 1],
                                                             in1=biasT.t[:, g * 128:(g + 1) * 128].unsqueeze(1).broadcast_to([128, 4, 128]),
                                                             op0=ALU.mult, op1=ALU.add), r=[pbank[b], lngT, biasT], w=[fC])
                    OP(DVE, lambda e: e.tensor_tensor(out=gB.t[:, g, :], in0=fC.t[:], in1=uT.t[:, g, :], op=ALU.mult), r=[fC, uT], w=[gB])
                u, ws = w_get()
                w_prefetch(u + 2)
                for blk in range(4):
                    b = fm_block(ws, wview(ws), blk)
                    OP(ACT, lambda e: e.activation(out=qMT.t[:, blk, :], in_=bank(b), func=AF.Copy), r=[pbank[b]], w=[qMT])
                w_release(u)
                u, ws = w_get()
                w_prefetch(u + 2)
                for blk in range(4):
                    b = fm_block(ws, wview(ws), blk)
                    silu2_evac(b, zMs.t[:, blk, :], zMs)
                w_release(u)
                for half in range(2):
                    b2 = next_bank()
                    for sj in range(8):
                        idx = half * 8 + sj
                        OP(PE, lambda e: e.transpose(out=bankb(b2)[:, sj * 128:(sj + 1) * 128], in_=PTt[:, idx * 128:(idx + 1) * 128],
                                                     identity=identb.t[:]), r=[PT[0], PT[1], identb], w=[pbank[b2]], sig=(sj == 7))
                    src = bankb(b2).rearrange("p (u j t) -> p u j t", u=2, j=4)
                    dstv = QT.t[:, :, half * 256:(half + 1) * 256].rearrange("p j (u t) -> p u j t", u=2)
                    OP(DVE, lambda e: e.tensor_copy(out=dstv, in_=src), r=[pbank[b2]], w=[QT])

                for h in range(4):
                    for kt in range(2):
                        b = next_bank()
                        OP(PE, lambda e: e.matmul(bank(b), lhsT=KmT.t[:, h, kt * 128:(kt + 1) * 128], rhs=qMT.t[:, h, :],
                                                  start=True, stop=True), r=[KmT, qMT], w=[pbank[b]], sig=True)
                        OP(ACT, lambda e: e.activation(out=PmT[kt].t[:], in_=bank(b), func=AF.Exp, scale=128 ** -0.5),
                           r=[pbank[b]], w=[PmT[kt]], sig=True)
                    bo = next_bank()
                    bl = next_bank()
                    for kt in range(2):
                        OP(PE, lambda e: e.matmul(bank(bo), lhsT=Vm.t[:, kt, h * 128:(h + 1) * 128], rhs=PmT[kt].t[:],
                                                  start=(kt == 0), stop=(kt == 1)), r=[Vm, PmT[kt]], w=[pbank[bo]])
                        OP(PE, lambda e: e.matmul(bank(bl), lhsT=onesb.t[:], rhs=PmT[kt].t[:],
                                                  start=(kt == 0), stop=(kt == 1)), r=[onesb, PmT[kt]], w=[pbank[bl]], sig=(kt == 1))
                    tm = tg[h % 2]
                    OP(DVE, lambda e: e.reciprocal(out=tm.t[:], in_=bank(bl)), r=[pbank[bl]], w=[tm])
                    OP(DVE, lambda e: e.tensor_tensor(out=tm.t[:], in0=bank(bo), in1=tm.t[:], op=ALU.mult), r=[pbank[bo], tm], w=[tm])
                    OP(POOL, lambda e: e.tensor_tensor(out=gMb.t[:, h, :], in0=tm.t[:], in1=zMs.t[:, h, :], op=ALU.mult),
                       r=[tm, zMs], w=[gMb])

                NKT = NT

                def qk(sub, kt, p):
                    for kv in range(2):
                        bb = 2 * p + kv
                        pr = slice(64 * kv, 64 * kv + 64)
                        OP(PE, lambda e: e.matmul(bank(bb), lhsT=KT.t[pr, kt * 128:(kt + 1) * 128],
                                                  rhs=QT.t[pr, :, sub * 128:(sub + 1) * 128], start=True, stop=True),
                           r=[KT, QT], w=[pbank[bb]], sig=(kv == 1))

                def ex(p, q):
                    OP(ACT, lambda e: e.activation(out=PT[q].t[:], in_=psum[:, p * 1024:(p + 1) * 1024], func=AF.Exp, scale=0.125),
                       r=[pbank[2 * p], pbank[2 * p + 1]], w=[PT[q]], sig=True)

                def pv(kt, p):
                    for kv in range(2):
                        for r_ in range(2):
                            pr = slice(64 * r_, 64 * r_ + 64)
                            bb = 4 + 2 * kv + r_
                            OP(PE, lambda e: e.matmul(bank(bb), lhsT=Vaug.t[pr, kt, kv, :], rhs=PT[p].t[pr, kv * 512:(kv + 1) * 512],
                                                      start=(kt == 0), stop=(kt == NKT - 1)),
                               r=[Vaug, PT[p]], w=[pbank[bb]], sig=(kt == NKT - 1 and kv == 1 and r_ == 1))

                def epilogue(sub):
                    for kv in range(2):
                        OP(DVE, lambda e: e.tensor_copy(out=tg[kv].t[:], in_=bank(4 + 2 * kv)), r=[pbank[4 + 2 * kv]], w=[tg[kv]])
                    for kv in range(2):
                        OP(DVE, lambda e: e.tensor_tensor(out=tg[kv].t[:], in0=tg[kv].t[:], in1=bank(5 + 2 * kv), op=ALU.add),
                           r=[tg[kv], pbank[5 + 2 * kv]], w=[tg[kv]])
                    OP(DVE, lambda e: e.reciprocal(out=fC.t[0:64, :], in_=tg[0].t[64:128, :]), r=[tg[0]], w=[fC])
                    OP(DVE, lambda e: e.reciprocal(out=fC.t[64:128, :], in_=tg[1].t[0:64, :]), r=[tg[1]], w=[fC])
                    OP(DVE, lambda e: e.tensor_tensor(out=fA.t[0:64, :], in0=tg[0].t[0:64, :], in1=fC.t[0:64, :], op=ALU.mult),
                       r=[tg[0], fC], w=[fA])
                    OP(DVE, lambda e: e.tensor_tensor(out=fA.t[64:128, :], in0=tg[1].t[64:128, :], in1=fC.t[64:128, :], op=ALU.mult),
                       r=[tg[1], fC], w=[fA])
                    OP(POOL, lambda e: e.tensor_tensor(out=gA.t[:, :, sub * 128:(sub + 1) * 128],
                                                       in0=fA.t[:].rearrange("p (j t) -> p j t", j=4),
                                                       in1=zAs.t[:, :, sub * 128:(sub + 1) * 128], op=ALU.mult), r=[fA, zAs], w=[gA])

                NI = 4 * NKT

                def it(i):
                    return divmod(i, NKT)
                qk(0, 0, 0)
                ex(0, 0)
                if NI > 1:
                    qk(*it(1), 1)
                for i in range(1, NI):
                    ex(i % 2, i % 3)
                    if i + 1 < NI:
                        qk(*it(i + 1), (i + 1) % 2)
                    psub, pkt = it(i - 1)
                    pv(pkt, (i - 1) % 3)
                    if pkt == NKT - 1:
                        epilogue(psub)
                    if i % 4 == min(2, NKT - 1):
                        stage_step()
                pv(NKT - 1, (NI - 1) % 3)
                epilogue(3)

                if s_ == NST - 1:
                    stage_flush()
                br = [gA, gB, gMb]
                nxt = s_ + 1 < NST
                if nxt:
                    pend = load_x(l, (s_ + 1) * 4)
                for cb in range(8):
                    hnext = None
                    if nxt and cb < 4:
                        curx = pend
                        if cb + 1 < 4:
                            pend = load_x(l, (s_ + 1) * 4 + cb + 1)
                        tile_n = (s_ + 1) * 4 + cb
                        hnext = hT_part1(curx.t[:], curx, gN, rstd.t[:, tile_n:tile_n + 1], rstd)
                    um = muse["n"]
                    wm = wms[um % 2]
                    m_prefetch(um + 1)
                    wmv = wm.t[:].rearrange("p (n e c) -> p n e c", n=3, e=12)
                    pu = [None, None, None]
                    for n in (1, 2, 0):
                        b = next_bank()
                        for k in range(8):
                            OP(PE, lambda e: e.matmul(bank(b), lhsT=wmv[:, n, k, :], rhs=hTc.t[:, k, :], start=(k == 0), stop=(k == 7)),
                               r=[wm, hTc], w=[pbank[b]], sig=(k == 7))
                        OP(ACT, lambda e: e.activation(out=tg[n].t[:], in_=bank(b), func=AF.Tanh, scale=0.5), r=[pbank[b]], w=[tg[n]], sig=True)
                        b = next_bank()
                        for kk in range(4):
                            OP(PE, lambda e: e.matmul(bank(b), lhsT=wmv[:, n, 8 + kk, :], rhs=br[n].t[:, kk, :], start=(kk == 0), stop=(kk == 3)),
                               r=[wm, br[n]], w=[pbank[b]], sig=(kk == 3))
                        pu[n] = b
                    muse["free"][um % 2] = True
                    muse["n"] += 1
                    OP(DVE, lambda e: e.scalar_tensor_tensor(out=fA.t[:], in0=tg[1].t[:], scalar=1.0, in1=bank(pu[1]),
                                                             op0=ALU.add, op1=ALU.mult), r=[tg[1], pbank[pu[1]]], w=[fA])
                    OP(DVE, lambda e: e.scalar_tensor_tensor(out=fC.t[:], in0=tg[2].t[:], scalar=1.0, in1=bank(pu[2]),
                                                             op0=ALU.add, op1=ALU.mult), r=[tg[2], pbank[pu[2]]], w=[fC])
                    OP(DVE, lambda e: e.tensor_tensor(out=fA.t[:], in0=fA.t[:], in1=fC.t[:], op=ALU.add), r=[fA, fC], w=[fA])
                    OP(DVE, lambda e: e.scalar_tensor_tensor(out=fC.t[:], in0=tg[0].t[:], scalar=1.0, in1=bank(pu[0]),
                                                             op0=ALU.add, op1=ALU.mult), r=[tg[0], pbank[pu[0]]], w=[fC])
                    OP(DVE, lambda e: e.tensor_tensor(out=mergedT.t[:, cb, :], in0=fA.t[:], in1=fC.t[:], op=ALU.add),
                       r=[fA, fC], w=[mergedT])
                    if hnext is not None:
                        hT_part2(hnext, hT[(s_ + 1) % 2], cb * 128)

                u0 = wuse["n"]
                w_prefetch(u0 + 2)
                wo = [wsl[u0 % 3], wsl[(u0 + 1) % 3]]
                for sub in range(4):
                    tile = s_ * 4 + sub
                    s = xr_ctr[0] % 2
                    xr_ctr[0] += 1
                    xt = xr[s]
                    DMA(st_xr[s], xt.t[:], x_src(l)[tile * 128:(tile + 1) * 128, :], w=[xt])
                    for half in range(2):
                        wov = wo[half].t[:].rearrange("p (k c) -> p k c", k=8)
                        b = next_bank()
                        for k in range(8):
                            OP(PE, lambda e: e.matmul(bank(b), lhsT=mergedT.t[:, k, sub * 128:(sub + 1) * 128], rhs=wov[:, k, :],
                                                      start=(k == 0), stop=(k == 7)), r=[wo[half], mergedT], w=[pbank[b]], sig=(k == 7))
                        OP(DVE, lambda e: e.scalar_tensor_tensor(out=xt.t[:, half * 512:(half + 1) * 512], in0=bank(b), scalar=0.5,
                                                                 in1=xt.t[:, half * 512:(half + 1) * 512], op0=ALU.mult, op1=ALU.add),
                           r=[pbank[b], xt], w=[xt])
                    OP(ACT, lambda e: e.activation(out=junk.t[:], in_=xt.t[:], func=AF.Square,
                                                   accum_out=ssq_nxt.t[:, tile:tile + 1]), r=[xt], w=[junk, ssq_nxt])
                    S.dma(ACT, st_xst[s], xres_d[tile * 128:(tile + 1) * 128, :], xt.t[:], reads=[xt.b], writes=[])
                w_release(u0)
                w_release(u0 + 1)
                wuse["n"] = u0 + 2
            for s in range(2):
                S.wait_stream(SP, st_xst[s])
            cur = 1 - cur


        S.new_epoch()
        ssq_cur = ssq[cur]
        DMA(st_tab, gN.t[:], fing_d.partition_broadcast(128), w=[gN])
        OP(DVE, lambda e: e.tensor_scalar(out=rstd.t[:], in0=ssq_cur.t[:], scalar1=1.0 / D, scalar2=EPS,
                                          op0=ALU.mult, op1=ALU.add), r=[ssq_cur], w=[rstd])
        OP(POOL, lambda e: e.tensor_tensor(out=rstd.t[:], in0=rstd.t[:], in1=mhalf.t[:, 0:NT], op=ALU.pow),
           r=[rstd, mhalf], w=[rstd])
        pend = load_x(1, 0)
        for i in range(NT):
            curx = pend
            if i + 1 < NT:
                pend = load_x(1, i + 1)
            s = i % 2
            OP(DVE, lambda e: e.scalar_tensor_tensor(out=xr[s].t[:], in0=curx.t[:], scalar=rstd.t[:, i:i + 1], in1=gN.t[:],
                                                     op0=ALU.mult, op1=ALU.mult), r=[curx, rstd, gN], w=[xr[s]])
            DMA(st_y, y_d[i * 128:(i + 1) * 128, :], xr[s].t[:], r=[xr[s]])
        S.wait_stream(SP, st_y)
        if debug:
            print("sbuf bytes remaining", nc.sbuf_bytes_remaining)
            print("ops", S.nops, "waits", S.nwaits, "sems", S.nsem)
    return nc


def _perm_weights(w_in, w_mem_kv, w_br, w_out):
    L = w_in.shape[0]
    out = np.empty((L, 128, WTOT), np.float32)
    for l in range(L):
        wi = w_in[l].reshape(8, 128, INW)
        o = out[l]
        o[:, OFF_KV:OFF_KV + 2048] = wi[:, :, 512:768].transpose(1, 0, 2).reshape(128, 2048)
        pair = np.concatenate([np.concatenate([np.arange(64) + 64 * j, 256 + np.arange(64) + 64 * j]) for j in range(4)])
        plain = np.arange(512)
        bases = [(0, pair), (768, pair), (1280, plain), (1792, plain), (2304, plain), (2816, plain), (3328, plain)]
        for g, (c0, cols) in enumerate(bases):
            blk = wi[:, :, c0 + cols].reshape(8, 128, 4, 128)
            o[:, OFF_G + g * 4096:OFF_G + (g + 1) * 4096] = blk.transpose(1, 2, 0, 3).reshape(128, 4096)
        wm = w_mem_kv[l].reshape(8, 128, 1024)
        o[:, OFF_MK:OFF_MK + 4096] = wm[:, :, 0:512].transpose(1, 0, 2).reshape(128, 4096)
        o[:, OFF_MV:OFF_MV + 4096] = wm[:, :, 512:1024].transpose(1, 0, 2).reshape(128, 4096)
        rowsA = np.stack([np.concatenate([64 * kk + np.arange(64), 256 + 64 * kk + np.arange(64)]) for kk in range(4)], 1)
        rowsP = np.stack([128 * kk + np.arange(128) for kk in range(4)], 1)
        for cb in range(8):
            unit = np.empty((128, 3, 12, 128), np.float32)
            for n in range(3):
                c0 = 3840 + n * 1024 + cb * 128
                unit[:, n, 0:8, :] = wi[:, :, c0:c0 + 128].transpose(1, 0, 2)
                rows = rowsA if n == 0 else rowsP
                unit[:, n, 8:12, :] = w_br[l, n][rows][:, :, cb * 128:(cb + 1) * 128]
            o[:, OFF_MG + cb * 4608:OFF_MG + (cb + 1) * 4608] = unit.reshape(128, 4608)
        wo = w_out[l].reshape(8, 128, 1024)
        for half in range(2):
            o[:, OFF_O + half * 4096:OFF_O + (half + 1) * 4096] = wo[:, :, half * 512:(half + 1) * 512].transpose(1, 0, 2).reshape(128, 4096)
    return out


def _rope_tables(seq):
    t = np.arange(seq)
    row = (t // 64).astype(np.float32)
    col = (t % 64).astype(np.float32)
    inv = (np.float32(10000.0) ** (-np.arange(16, dtype=np.float32) / np.float32(16))).astype(np.float32)
    ang = np.stack([row[:, None] * inv, col[:, None] * inv], axis=1).astype(np.float32)
    nt = seq // 128
    c = np.cos(ang).astype(np.float32).reshape(nt, 128, 32).transpose(1, 0, 2).reshape(128, nt * 32)
    s = np.sin(ang).astype(np.float32).reshape(nt, 128, 32).transpose(1, 0, 2).reshape(128, nt * 32)
    return np.ascontiguousarray(c), np.ascontiguousarray(s)


def make_in_maps(inp, L=None):
    x = np.asarray(inp["x"], np.float32)
    B, seq, _ = x.shape
    L = L or inp["w_in"].shape[0]
    wflat = _perm_weights(np.asarray(inp["w_in"])[:L], np.asarray(inp["w_mem_kv"])[:L],
                          np.asarray(inp["w_br"])[:L], np.asarray(inp["w_out"])[:L])
    c, s = _rope_tables(seq)
    wsT = np.ascontiguousarray(np.asarray(inp["w_s"])[:L].transpose(0, 3, 1, 2).reshape(L, 128, 512))
    common = {
        "wflat": wflat,
        "norm_g": np.ascontiguousarray(inp["norm_g"][:L]), "q_norm_g": np.ascontiguousarray(inp["q_norm_g"][:L]),
        "k_norm_g": np.ascontiguousarray(inp["k_norm_g"][:L]), "lngT": np.ascontiguousarray(np.asarray(inp["sg_ln_g"])[:L].reshape(L, 4, 128).transpose(0, 2, 1)),
        "lnbT": np.ascontiguousarray(np.asarray(inp["sg_ln_b"])[:L].reshape(L, 4, 128).transpose(0, 2, 1)), "wsT": wsT,
        "b_s": np.ascontiguousarray(np.asarray(inp["b_s"])[:L].reshape(L, 512)),
        "mem_norm_g": np.ascontiguousarray(inp["mem_norm_g"][:L]), "final_g": np.ascontiguousarray(inp["final_g"]),
        "ropecos": c, "ropesin": s, "ident": np.eye(128, dtype=np.float32),
    }
    maps = []
    for b in range(B):
        m = dict(common)
        m["x"] = np.ascontiguousarray(x[b])
        m["mem"] = np.ascontiguousarray(np.asarray(inp["mem"], np.float32)[b])
        maps.append(m)
    return maps


def kernel(**inputs):
    x = np.asarray(inputs["x"])
    B, seq, _ = x.shape
    L = inputs["w_in"].shape[0]
    in_maps = make_in_maps(inputs)
    nc = build(NT=seq // 128, L=L)
    res = run_bass_kernel_spmd(nc, in_maps, core_ids=list(range(B)))
    return np.stack([np.asarray(r["y"]) for r in res.results], axis=0).astype(np.float32)
```
